# Optimizing a Trainium2 kernel written in Bass

```python
import jax
import jax.numpy as jnp
from jax import lax
import numpy as np

D_MODEL = 2048
BATCH = 1
SEQ = 8192
DEPTH = 4

A_HEADS = 8
A_KV_GROUPS = 2
A_HPG = A_HEADS // A_KV_GROUPS
HEAD_DIM = 128
A_WIDTH = A_HEADS * HEAD_DIM
A_KV_WIDTH = A_KV_GROUPS * HEAD_DIM
CMP_BLOCK = 32
CMP_STRIDE = 16
SEL_BLOCK = 64
SEL_TOPK = 16
WINDOW = 512
Q_BLOCK = 128
ROPE_THETA = 10000.0
HG_HEADS = 8
HG_DIM = 128
HG_WIDTH = HG_HEADS * HG_DIM
HG_CHUNK = 64
D_FF = 5632
CONV_WIDTH = 3
NORM_EPS = 1e-6
N_ADA = 6
IN_SIZES = (A_WIDTH, 6 * A_KV_WIDTH, 3 * A_HEADS, HG_WIDTH, HG_WIDTH, HG_WIDTH, HG_WIDTH, 2 * D_MODEL)
IN_COLS = A_WIDTH + 6 * A_KV_WIDTH + 3 * A_HEADS + 4 * HG_WIDTH + 2 * D_MODEL

kernel_name = 'hybrid_nsa_hgrn2_convffn_adaln_trunk'


def split_cols(z, sizes):
    out = []
    off = 0
    for s in sizes:
        out.append(z[..., off:off + s])
        off += s
    return out


def rms_norm(t, g):
    tf = t.astype(jnp.float32)
    y = tf * lax.rsqrt(jnp.mean(tf * tf, axis=-1, keepdims=True) + NORM_EPS)
    return (y * g.astype(jnp.float32)).astype(t.dtype)


def rope_tables(positions):
    inv_freq = 1.0 / (ROPE_THETA ** (jnp.arange(0, HEAD_DIM, 2, dtype=jnp.float32) / HEAD_DIM))
    ang = positions.astype(jnp.float32)[..., None] * inv_freq
    return jnp.cos(ang)[:, :, None, :], jnp.sin(ang)[:, :, None, :]


def apply_rope(t, cos, sin):
    tf = t.astype(jnp.float32)
    t1, t2 = jnp.split(tf, 2, axis=-1)
    return jnp.concatenate([t1 * cos - t2 * sin, t2 * cos + t1 * sin], axis=-1).astype(t.dtype)


def compress_blocks(t, pos_emb, w):
    B, S, G, hd = t.shape
    n = (S - CMP_BLOCK) // CMP_STRIDE + 1
    idx = jnp.arange(n)[:, None] * CMP_STRIDE + jnp.arange(CMP_BLOCK)[None, :]
    blocks = t[:, idx] + pos_emb[None, None, :, None, :]
    blocks = blocks.transpose(0, 1, 3, 2, 4).reshape(B, n, G, CMP_BLOCK * hd)
    return blocks @ w


def masked_softmax(s, mask):
    s = jnp.where(mask, s, -jnp.inf)
    m = jnp.max(s, axis=-1, keepdims=True)
    m = jnp.where(jnp.isfinite(m), m, 0.0)
    p = jnp.exp(s - m)
    den = jnp.sum(p, axis=-1, keepdims=True)
    return p / jnp.where(den > 0, den, 1.0)


def nsa_sequence(q, k_cmp, v_cmp, k_sel, v_sel, k_win, v_win, gates):
    S = q.shape[0]
    n_cmp = k_cmp.shape[0]
    n_sel = S // SEL_BLOCK
    topk = min(SEL_TOPK, n_sel)
    scale = HEAD_DIM ** -0.5
    cmp_start = jnp.arange(n_cmp) * CMP_STRIDE
    cmp_last = cmp_start + CMP_BLOCK - 1
    sel_start = jnp.arange(n_sel) * SEL_BLOCK
    overlap = jnp.maximum(
        jnp.minimum(cmp_start[:, None] + CMP_BLOCK, sel_start[None, :] + SEL_BLOCK)
        - jnp.maximum(cmp_start[:, None], sel_start[None, :]), 0).astype(jnp.float32) / CMP_BLOCK
    ks_blk = k_sel.reshape(n_sel, SEL_BLOCK, A_KV_GROUPS, HEAD_DIM).transpose(2, 0, 1, 3)
    vs_blk = v_sel.reshape(n_sel, SEL_BLOCK, A_KV_GROUPS, HEAD_DIM).transpose(2, 0, 1, 3)
    kw_pad = jnp.pad(k_win, ((WINDOW, 0), (0, 0), (0, 0)))
    vw_pad = jnp.pad(v_win, ((WINDOW, 0), (0, 0), (0, 0)))
    blk_ids = jnp.arange(n_sel)
    in_blk = jnp.arange(SEL_BLOCK)
    win_off = jnp.arange(Q_BLOCK + WINDOW) - WINDOW

    def one_block(bi):
        t0 = bi * Q_BLOCK
        tpos = t0 + jnp.arange(Q_BLOCK)
        qb = lax.dynamic_slice_in_dim(q, t0, Q_BLOCK, 0).reshape(Q_BLOCK, A_KV_GROUPS, A_HPG, HEAD_DIM) * scale
        gb = lax.dynamic_slice_in_dim(gates, t0, Q_BLOCK, 0).reshape(Q_BLOCK, A_KV_GROUPS, A_HPG, 3)
        s_c = jnp.einsum('qghd,ngd->ghqn', qb, k_cmp)
        p_c = masked_softmax(s_c, (cmp_last[None, :] <= tpos[:, None])[None, None])
        o_c = jnp.einsum('ghqn,ngd->qghd', p_c, v_cmp)
        imp = jnp.einsum('ghqn,nm->gqm', p_c, overlap)
        cur = tpos // SEL_BLOCK
        forced = (blk_ids[None] == 0) | (blk_ids[None] == cur[:, None]) | (blk_ids[None] == cur[:, None] - 1)
        valid = sel_start[None] <= tpos[:, None]
        imp = jnp.where(forced[None], jnp.inf, jnp.where(valid[None], imp, -jnp.inf))
        _, idx = lax.top_k(imp, topk)
        kg = jax.vmap(lambda kb, ix: kb[ix])(ks_blk, idx).reshape(A_KV_GROUPS, Q_BLOCK, topk * SEL_BLOCK, HEAD_DIM)
        vg = jax.vmap(lambda vb, ix: vb[ix])(vs_blk, idx).reshape(A_KV_GROUPS, Q_BLOCK, topk * SEL_BLOCK, HEAD_DIM)
        kpos = (idx[..., None] * SEL_BLOCK + in_blk).reshape(A_KV_GROUPS, Q_BLOCK, topk * SEL_BLOCK)
        s_s = jnp.einsum('qghd,gqkd->ghqk', qb, kg)
        p_s = masked_softmax(s_s, (kpos <= tpos[None, :, None])[:, None])
        o_s = jnp.einsum('ghqk,gqkd->qghd', p_s, vg)
        kwb = lax.dynamic_slice_in_dim(kw_pad, t0, Q_BLOCK + WINDOW, 0)
        vwb = lax.dynamic_slice_in_dim(vw_pad, t0, Q_BLOCK + WINDOW, 0)
        wpos = t0 + win_off
        dist = tpos[:, None] - wpos[None, :]
        mask_w = (dist >= 0) & (dist < WINDOW) & (wpos[None, :] >= 0)
        s_w = jnp.einsum('qghd,kgd->ghqk', qb, kwb)
        p_w = masked_softmax(s_w, mask_w[None, None])
        o_w = jnp.einsum('ghqk,kgd->qghd', p_w, vwb)
        o = gb[..., 0:1] * o_c + gb[..., 1:2] * o_s + gb[..., 2:3] * o_w
        return o.reshape(Q_BLOCK, A_WIDTH)

    out = lax.map(one_block, jnp.arange(S // Q_BLOCK))
    return out.reshape(S, A_WIDTH)


def hgrn2_recurrence(q, f_pre, v, lb):
    B, S, H, d = q.shape
    lb = lb.reshape(H, d)
    log_f = jnp.logaddexp(jnp.log(lb), jnp.log1p(-lb) + jax.nn.log_sigmoid(f_pre))
    k = -jnp.expm1(log_f)
    nc = S // HG_CHUNK

    def to_chunks(t):
        return t.reshape(B, nc, HG_CHUNK, H, t.shape[-1]).transpose(1, 0, 3, 2, 4)

    causal = jnp.tril(jnp.ones((HG_CHUNK, HG_CHUNK), dtype=bool))[:, :, None]

    def step(state, inp):
        qc, kc, vc, gc = inp
        b = jnp.cumsum(gc, axis=2)
        diff = b[:, :, :, None, :] - b[:, :, None, :, :]
        decay = jnp.where(causal, jnp.exp(jnp.where(causal, diff, 0.0)), 0.0)
        scores = jnp.einsum('bhtd,bhsd,bhtsd->bhts', qc, kc, decay)
        o = jnp.einsum('bhts,bhsv->bhtv', scores, vc) + jnp.einsum('bhtd,bhdv->bhtv', qc * jnp.exp(b), state)
        b_last = b[:, :, -1:, :]
        state = jnp.exp(b_last[:, :, 0, :])[..., None] * state + jnp.einsum('bhsd,bhsv->bhdv', kc * jnp.exp(b_last - b), vc)
        return state, o

    state0 = jnp.zeros((B, H, d, v.shape[-1]), jnp.float32)
    _, o = lax.scan(step, state0, (to_chunks(q), to_chunks(k), to_chunks(v), to_chunks(log_f)))
    return o.transpose(1, 0, 3, 2, 4).reshape(B, S, H, v.shape[-1])


def causal_depthwise_conv(u, w, b):
    S = u.shape[1]
    u_pad = jnp.pad(u, ((0, 0), (CONV_WIDTH - 1, 0), (0, 0)))
    out = b + w[0] * u_pad[:, 0:S]
    for j in range(1, CONV_WIDTH):
        out = out + w[j] * u_pad[:, j:j + S]
    return out


def setup_inputs(seed: int = 0) -> dict:
    key = jax.random.key(seed)
    k = jax.random.split(key, 24)

    def nrm(kk, shape, scale):
        return jax.random.normal(kk, shape, jnp.float32) * scale

    L = DEPTH
    D = D_MODEL
    return {
        'x': nrm(k[0], (BATCH, SEQ, D), 1.0),
        'c': nrm(k[1], (BATCH, D), 1.0),
        'positions': jnp.broadcast_to(jnp.arange(SEQ, dtype=jnp.int32), (BATCH, SEQ)),
        'w_ada': nrm(k[2], (L, D, N_ADA * D), 0.5 * D ** -0.5),
        'b_ada': nrm(k[3], (L, N_ADA * D), 0.02),
        'g_pre_mix': 1.0 + nrm(k[4], (L, D), 0.05),
        'w_in': nrm(k[5], (L, D, IN_COLS), D ** -0.5),
        'pe_kc': nrm(k[6], (L, CMP_BLOCK, HEAD_DIM), 0.1),
        'w_kc': nrm(k[7], (L, CMP_BLOCK * HEAD_DIM, HEAD_DIM), (CMP_BLOCK * HEAD_DIM) ** -0.5),
        'pe_vc': nrm(k[8], (L, CMP_BLOCK, HEAD_DIM), 0.1),
        'w_vc': nrm(k[9], (L, CMP_BLOCK * HEAD_DIM, HEAD_DIM), (CMP_BLOCK * HEAD_DIM) ** -0.5),
        'lb_logits': nrm(k[10], (L, HG_WIDTH), 1.0),
        'g_hg_norm': 1.0 + nrm(k[11], (L, HG_WIDTH), 0.05),
        'w_br_attn': nrm(k[12], (L, A_WIDTH, D), A_WIDTH ** -0.5),
        'w_br_hgrn': nrm(k[13], (L, HG_WIDTH, D), HG_WIDTH ** -0.5),
        'w_out': nrm(k[14], (L, D, D), D ** -0.5),
        'g_post_mix': 1.0 + nrm(k[15], (L, D), 0.05),
        'g_pre_ffn': 1.0 + nrm(k[16], (L, D), 0.05),
        'w_up': nrm(k[17], (L, D, 2 * D_FF), D ** -0.5),
        'conv_w': nrm(k[18], (L, CONV_WIDTH, 2 * D_FF), CONV_WIDTH ** -0.5),
        'conv_b': nrm(k[19], (L, 2 * D_FF), 0.02),
        'w_down': nrm(k[20], (L, D_FF, D), D_FF ** -0.5),
        'g_post_ffn': 1.0 + nrm(k[21], (L, D), 0.05),
    }


def reference(x, c, positions, w_ada, b_ada, g_pre_mix, w_in, pe_kc, w_kc, pe_vc, w_vc, lb_logits, g_hg_norm,
              w_br_attn, w_br_hgrn, w_out, g_post_mix, g_pre_ffn, w_up, conv_w, conv_b, w_down, g_post_ffn):
    B, S, D = x.shape
    dt = x.dtype
    cos, sin = rope_tables(positions)
    lb_cum = jnp.cumsum(jax.nn.softmax(lb_logits.astype(jnp.float32), axis=0), axis=0)
    lower_bounds = lb_cum - lb_cum[0:1]
    c_act = jax.nn.silu(c)

    def f32(t):
        return t.astype(jnp.float32)

    for l in range(DEPTH):
        ada = (c_act @ w_ada[l] + b_ada[l])[:, None, :]
        sh1, sc1, gt1, sh2, sc2, gt2 = jnp.split(ada, N_ADA, axis=-1)

        h = rms_norm(x, g_pre_mix[l]) * (1.0 + sc1) + sh1
        z = h @ w_in[l]
        q_a, kv, a_gate, q_h, f_h, i_h, og_h, m_gate = split_cols(z, IN_SIZES)

        q_a = apply_rope(q_a.reshape(B, S, A_HEADS, HEAD_DIM), cos, sin)
        kc_raw, vc_raw, k_sel, v_sel, k_win, v_win = [
            t.reshape(B, S, A_KV_GROUPS, HEAD_DIM) for t in jnp.split(kv, 6, axis=-1)]
        kc_raw = apply_rope(kc_raw, cos, sin)
        k_sel = apply_rope(k_sel, cos, sin)
        k_win = apply_rope(k_win, cos, sin)
        k_cmp = compress_blocks(kc_raw, pe_kc[l], w_kc[l])
        v_cmp = compress_blocks(vc_raw, pe_vc[l], w_vc[l])
        branch_gates = jax.nn.sigmoid(f32(a_gate)).reshape(B, S, A_HEADS, 3)
        attn = jax.vmap(nsa_sequence)(f32(q_a), f32(k_cmp), f32(v_cmp), f32(k_sel), f32(v_sel),
                                      f32(k_win), f32(v_win), branch_gates).astype(dt)

        hg = hgrn2_recurrence(f32(q_h).reshape(B, S, HG_HEADS, HG_DIM),
                              f32(f_h).reshape(B, S, HG_HEADS, HG_DIM),
                              f32(i_h).reshape(B, S, HG_HEADS, HG_DIM),
                              lower_bounds[l]).astype(dt)
        hg = rms_norm(hg, g_hg_norm[l].reshape(HG_HEADS, HG_DIM)) * jax.nn.silu(og_h.reshape(B, S, HG_HEADS, HG_DIM))
        hg = hg.reshape(B, S, HG_WIDTH)

        g_attn, g_hgrn = jnp.split(jax.nn.sigmoid(m_gate), 2, axis=-1)
        mixed = g_attn * (attn @ w_br_attn[l]) + g_hgrn * (hg @ w_br_hgrn[l])
        x = x + gt1 * rms_norm(mixed @ w_out[l], g_post_mix[l])

        h = rms_norm(x, g_pre_ffn[l]) * (1.0 + sc2) + sh2
        u = causal_depthwise_conv(h @ w_up[l], conv_w[l], conv_b[l])
        u_gate, u_val = jnp.split(u, 2, axis=-1)
        x = x + gt2 * rms_norm((jax.nn.silu(u_gate) * u_val) @ w_down[l], g_post_ffn[l])
    return x
```

```python
import numpy as np
from contextlib import ExitStack
import concourse.bass as bass
import concourse.mybir as mybir
from concourse.bass_utils import run_bass_kernel_spmd

F32 = mybir.dt.float32
F32R = mybir.dt.float32r
BF16 = mybir.dt.bfloat16
I32 = mybir.dt.int32
AF = mybir.ActivationFunctionType
ALU = mybir.AluOpType
AX = mybir.AxisListType


class Buf:
    def __init__(self, kb, t, name):
        self.kb = kb
        self.t = t
        self.name = name
        self.w = {}
        self.r = {}
        self.dsem = None
        self.dval = 0
        self.excl = False

    def __getitem__(self, k):
        return self.t[k]

    def ap(self):
        return self.t[:]


class KB:
    def __init__(self):
        self.nc = bass.Bass("TRN2", target_bir_lowering=False)
        nc = self.nc
        self.es = ExitStack()
        self.eng = {"pe": nc.tensor, "act": nc.scalar, "dve": nc.vector, "pool": nc.gpsimd, "sp": nc.sync}
        self.sem = {e: self.es.enter_context(nc.semaphore("s_" + e)) for e in self.eng}
        self.cnt = {e: 0 for e in self.eng}
        self.seen = {e: {} for e in self.eng}
        self.outs = []
        self.nsem = 0

    def sb(self, name, shape, dt=F32):
        t = self.es.enter_context(self.nc.sbuf_tensor(name, list(shape), dt))
        return Buf(self, t, name)

    def ps(self, name, shape, dt=F32):
        t = self.es.enter_context(self.nc.psum_tensor(name, list(shape), dt))
        b = Buf(self, t, name)
        b.excl = True
        return b

    def view(self, t, name):
        return Buf(self, t, name)

    def din(self, name, shape, dt=F32):
        t = self.nc.dram_tensor(name, list(shape), dt, kind="ExternalInput").ap()
        return Buf(self, t, name)

    def dout(self, name, shape, dt=F32):
        t = self.nc.dram_tensor(name, list(shape), dt, kind="ExternalOutput").ap()
        b = Buf(self, t, name)
        self.outs.append(b)
        return b

    def dsem_of(self, b):
        if b.dsem is None:
            self.nsem += 1
            b.dsem = self.es.enter_context(self.nc.semaphore("d%d" % self.nsem))
        return b.dsem

    def _wait(self, e, key, sem, val):
        if self.seen[e].get(key, 0) >= val:
            return
        self.eng[e].wait_ge(sem, val)
        self.seen[e][key] = val

    def _deps(self, e, own, reads, writes):
        for b in reads:
            for key, (sem, val) in b.w.items():
                self._wait(e, key, sem, val)
            if b.excl:
                for key, (sem, val) in b.r.items():
                    if key != own:
                        self._wait(e, key, sem, val)
        for b in writes:
            for key, (sem, val) in list(b.w.items()) + list(b.r.items()):
                if key == own:
                    continue
                self._wait(e, key, sem, val)

    def _mark(self, key, tok, reads, writes, partial):
        for b in reads:
            b.r[key] = tok
        for b in writes:
            if partial:
                b.w[key] = tok
            else:
                b.w = {key: tok}
                b.r = {}

    def op(self, e, emit, reads=(), writes=(), inc=True, partial=False):
        self._deps(e, e, reads, writes)
        ins = emit(self.eng[e])
        if inc:
            self.cnt[e] += 1
            ins.then_inc(self.sem[e], 1)
            tokv = self.cnt[e]
        else:
            tokv = self.cnt[e] + 1
        self._mark(e, (self.sem[e], tokv), reads, writes, partial)
        return ins

    def dma(self, q, out_ap, in_ap, reads=(), writes=(), partial=False, sbuf_side=None):
        sb = sbuf_side if sbuf_side is not None else (writes[0] if writes else reads[0])
        sem = self.dsem_of(sb)
        key = ("d", id(sb))
        self._deps(q, key, reads, writes)
        ins = self.eng[q].dma_start(out=out_ap, in_=in_ap)
        sb.dval += 16
        ins.then_inc(sem, 16)
        self._mark(key, (sem, sb.dval), reads, writes, partial)
        return ins

    def finish(self):
        for b in self.outs:
            for key, (sem, val) in b.w.items():
                self._wait("sp", key, sem, val)
        for e in ("pe", "act", "dve", "pool"):
            if self.cnt[e]:
                self._wait("sp", e, self.sem[e], self.cnt[e])

    def run(self, in_maps, trace=False):
        res = run_bass_kernel_spmd(self.nc, in_maps, core_ids=list(range(len(in_maps))), trace=trace)
        return res


import math
import os
import numpy as np

D = 2048
TPC = 1024
NT = TPC // 128
PI = math.pi


def build_L0():
    kb = KB()
    c2 = kb.din("c2", [128, 16])
    wada = kb.din("wada", [4, 2048, 1536])
    bada = kb.din("bada", [1, 4 * 1536])
    lbl = kb.din("lbl", [128, 4, 8])
    pos = kb.din("pos", [128, 8], I32)
    invf = kb.din("invf", [128, 64])
    ada_o = kb.dout("ada_o", [1, 4 * 1536])
    lb_o = kb.dout("lb_o", [128, 4, 8])
    cos_o = kb.dout("cos_o", [1024, 64])
    sin_o = kb.dout("sin_o", [1024, 64])

    cs = kb.sb("cs", [128, 16])
    cact = kb.sb("cact", [128, 16])
    bs = kb.sb("bs", [1, 4 * 1536])
    ws = [kb.sb("ws%d" % i, [128, 16, 512]) for i in range(2)]
    pss = [kb.ps("ps%d" % i, [1, 512]) for i in range(2)]
    ao = kb.sb("ao", [1, 4 * 1536])
    kb.dma("sp", cs[:], c2[:], reads=[c2], writes=[cs])
    kb.dma("sp", bs[:], bada[:], reads=[bada], writes=[bs])
    kb.op("act", lambda e: e.activation(out=cact[:], in_=cs[:], func=AF.Silu), reads=[cs], writes=[cact])
    i = 0
    for l in range(4):
        for cb in range(3):
            w = ws[i % 2]
            p = pss[i % 2]
            src = wada[l, :, cb * 512:(cb + 1) * 512].rearrange("(kc p) n -> p kc n", p=128)
            kb.dma("sp" if i % 2 == 0 else "act", w[:], src, reads=[wada], writes=[w])
            for kc in range(16):
                kb.op("pe", lambda e, kc=kc, w=w, p=p: e.matmul(p[:], cact[:, kc:kc + 1], w[:, kc, :], start=(kc == 0), stop=(kc == 15)),
                      reads=[cact, w], writes=[p], inc=(kc == 15))
            o0 = l * 1536 + cb * 512
            kb.op("dve", lambda e, p=p, o0=o0: e.tensor_tensor(out=ao[:, o0:o0 + 512], in0=p[:], in1=bs[:, o0:o0 + 512], op=ALU.add),
                  reads=[p, bs], writes=[ao], partial=True)
            i += 1
    kb.dma("sp", ada_o[:], ao[:], reads=[ao], writes=[ada_o], sbuf_side=ao)

    ll = kb.sb("ll", [128, 4, 8])
    le = kb.sb("le", [128, 4, 8])
    lsum = kb.sb("lsum", [128, 8])
    lo = kb.sb("lo", [128, 4, 8])
    kb.dma("sp", ll[:], lbl[:], reads=[lbl], writes=[ll])
    kb.op("act", lambda e: e.activation(out=le[:], in_=ll[:], func=AF.Exp), reads=[ll], writes=[le])
    kb.op("dve", lambda e: e.tensor_tensor(out=lsum[:], in0=le[:, 0, :], in1=le[:, 1, :], op=ALU.add), reads=[le], writes=[lsum])
    kb.op("dve", lambda e: e.tensor_tensor(out=lsum[:], in0=lsum[:], in1=le[:, 2, :], op=ALU.add), reads=[le, lsum], writes=[lsum])
    kb.op("dve", lambda e: e.tensor_tensor(out=lsum[:], in0=lsum[:], in1=le[:, 3, :], op=ALU.add), reads=[le, lsum], writes=[lsum])
    kb.op("dve", lambda e: e.reciprocal(out=lsum[:], in_=lsum[:]), reads=[lsum], writes=[lsum])
    kb.op("dve", lambda e: e.memset(lo[:, 0, :], 0.0), writes=[lo])
    for l in range(1, 4):
        kb.op("dve", lambda e, l=l: e.tensor_tensor(out=le[:, l, :], in0=le[:, l, :], in1=lsum[:], op=ALU.mult), reads=[le, lsum], writes=[le])
        kb.op("dve", lambda e, l=l: e.tensor_tensor(out=lo[:, l, :], in0=lo[:, l - 1, :], in1=le[:, l, :], op=ALU.add), reads=[le, lo], writes=[lo])
    kb.dma("sp", lb_o[:], lo[:], reads=[lo], writes=[lb_o], sbuf_side=lo)

    pi_ = kb.sb("pi_", [128, 8], I32)
    pf = kb.sb("pf", [128, 8])
    ivf = kb.sb("ivf", [128, 64])
    ang = kb.sb("ang", [128, 8, 64])
    m1 = kb.sb("m1", [128, 8, 64])
    m2 = kb.sb("m2", [128, 8, 64])
    so = kb.sb("so", [128, 8, 64])
    co = kb.sb("co", [128, 8, 64])
    negpi = kb.sb("negpi", [128, 1])
    kb.op("dve", lambda e: e.memset(negpi[:], -PI), writes=[negpi])
    kb.dma("sp", pi_[:], pos[:], reads=[pos], writes=[pi_])
    kb.dma("sp", ivf[:], invf[:], reads=[invf], writes=[ivf])
    kb.op("dve", lambda e: e.tensor_copy(out=pf[:], in_=pi_[:]), reads=[pi_], writes=[pf])
    for t in range(8):
        kb.op("dve", lambda e, t=t: e.tensor_scalar(out=ang[:, t, :], in0=ivf[:], scalar1=pf[:, t:t + 1], scalar2=0.0, op0=ALU.mult, op1=ALU.add),
              reads=[ivf, pf], writes=[ang], partial=True)
    C1 = 6.28125
    C2 = 2 * PI - C1
    PIC = 3.1415925
    ki = kb.sb("ki", [128, 8, 64], I32)
    kf = kb.sb("kf", [128, 8, 64])
    wr = kb.sb("wr", [128, 8, 64])
    kb.op("dve", lambda e: e.tensor_scalar(out=m1[:], in0=ang[:], scalar1=1.0 / (2 * PI), scalar2=0.0, op0=ALU.mult, op1=ALU.add), reads=[ang], writes=[m1])
    kb.op("dve", lambda e: e.tensor_copy(out=ki[:], in_=m1[:]), reads=[m1], writes=[ki])
    kb.op("dve", lambda e: e.tensor_copy(out=kf[:], in_=ki[:]), reads=[ki], writes=[kf])
    kb.op("dve", lambda e: e.scalar_tensor_tensor(out=m1[:], in0=kf[:], scalar=-C1, in1=ang[:], op0=ALU.mult, op1=ALU.add), reads=[kf, ang], writes=[m1])
    kb.op("dve", lambda e: e.scalar_tensor_tensor(out=m1[:], in0=kf[:], scalar=-C2, in1=m1[:], op0=ALU.mult, op1=ALU.add), reads=[kf, m1], writes=[m1])

    def wrap(buf):
        kb.op("dve", lambda e: e.tensor_scalar(out=wr[:], in0=buf[:], scalar1=PI, scalar2=-2 * PI, op0=ALU.is_gt, op1=ALU.mult), reads=[buf], writes=[wr])
        kb.op("dve", lambda e: e.tensor_tensor(out=buf[:], in0=buf[:], in1=wr[:], op=ALU.add), reads=[buf, wr], writes=[buf])
        kb.op("dve", lambda e: e.tensor_scalar(out=wr[:], in0=buf[:], scalar1=-PI, scalar2=2 * PI, op0=ALU.is_lt, op1=ALU.mult), reads=[buf], writes=[wr])
        kb.op("dve", lambda e: e.tensor_tensor(out=buf[:], in0=buf[:], in1=wr[:], op=ALU.add), reads=[buf, wr], writes=[buf])
        kb.op("dve", lambda e: e.tensor_scalar(out=buf[:], in0=buf[:], scalar1=-PIC, scalar2=PIC, op0=ALU.max, op1=ALU.min), reads=[buf], writes=[buf])
    wrap(m1)
    kb.op("dve", lambda e: e.tensor_scalar(out=m2[:], in0=m1[:], scalar1=0.5 * PI, scalar2=0.0, op0=ALU.add, op1=ALU.add), reads=[m1], writes=[m2])
    wrap(m2)
    kb.op("act", lambda e: e.activation(out=so[:], in_=m1[:], func=AF.Sin), reads=[m1], writes=[so])
    kb.op("act", lambda e: e.activation(out=co[:], in_=m2[:], func=AF.Sin), reads=[m2], writes=[co])
    kb.dma("sp", sin_o.t.rearrange("(t p) f -> p t f", p=128), so[:], reads=[so], writes=[sin_o], sbuf_side=so)
    kb.dma("sp", cos_o.t.rearrange("(t p) f -> p t f", p=128), co[:], reads=[co], writes=[cos_o], sbuf_side=co)
    kb.finish()
    return kb


def run_L0(inp):
    kb = build_L0()
    c = np.asarray(inp["c"], np.float32)
    inv_freq = (1.0 / (10000.0 ** (np.arange(0, 128, 2, dtype=np.float32) / np.float32(128)))).astype(np.float32)
    maps = []
    for j in range(8):
        maps.append({
            "c2": np.ascontiguousarray(c.reshape(16, 128).T),
            "wada": np.ascontiguousarray(inp["w_ada"][:, :, j * 1536:(j + 1) * 1536]),
            "bada": np.ascontiguousarray(inp["b_ada"][:, j * 1536:(j + 1) * 1536]).reshape(1, -1),
            "lbl": np.ascontiguousarray(inp["lb_logits"].reshape(4, 8, 128).transpose(2, 0, 1)),
            "pos": np.ascontiguousarray(inp["positions"][0, j * 1024:(j + 1) * 1024].reshape(8, 128).T.astype(np.int32)),
            "invf": np.ascontiguousarray(np.broadcast_to(inv_freq[None, :], (128, 64))),
        })
    res = kb.run(maps).results
    ada = np.concatenate([r["ada_o"].reshape(4, 1536) for r in res], axis=1)
    lb = res[0]["lb_o"].transpose(1, 2, 0).reshape(4, 1024)
    cos = np.concatenate([r["cos_o"] for r in res], axis=0)
    sin = np.concatenate([r["sin_o"] for r in res], axis=0)
    return ada, lb, cos, sin


class Eng2:
    def __init__(self, names):
        self.names = names
        self.i = 0

    def next(self):
        n = self.names[self.i % len(self.names)]
        self.i += 1
        return n


def copy_op(kb, eng, out_ap, in_ap, reads, writes, partial=False):
    if eng == "act":
        return kb.op("act", lambda e: e.activation(out=out_ap, in_=in_ap, func=AF.Copy), reads=reads, writes=writes, partial=partial)
    return kb.op(eng, lambda e: e.tensor_copy(out=out_ap, in_=in_ap), reads=reads, writes=writes, partial=partial)


def bcast_load(kb, q, dst, src_row_ap, src_buf):
    kb.dma(q, dst[:], src_row_ap.partition_broadcast(128), reads=[src_buf], writes=[dst])


def rms_rstd(kb, xt, np_, ncols, junk, ss, epsb, rstd):
    kb.op("act", lambda e: e.activation(out=junk[:np_, :ncols], in_=xt[:np_, :ncols], func=AF.Square, accum_out=ss[:np_, :]),
          reads=[xt], writes=[junk, ss])
    kb.op("act", lambda e: e.activation(out=rstd[:np_, :], in_=ss[:np_, :], func=AF.Sqrt, bias=epsb[:np_, :], scale=1.0 / ncols),
          reads=[ss, epsb], writes=[rstd])
    kb.op("dve", lambda e: e.reciprocal(out=rstd[:np_, :], in_=rstd[:np_, :]), reads=[rstd], writes=[rstd])


def transpose_into(kb, src, np_, nchunks, dstT, ident, tps, ev, dst_col0=0):
    for g in range(0, nchunks, 4):
        n = min(4, nchunks - g)
        tp = tps.next()
        for j in range(n):
            kc = g + j
            kb.op("pe", lambda e, kc=kc, j=j, tp=tp: e.transpose(tp[:, j * 128:j * 128 + np_], src[:np_, kc * 128:(kc + 1) * 128], ident[:np_, :np_]),
                  reads=[src, ident], writes=[tp], inc=(j == n - 1), partial=(j > 0))
        tv = tp[:, 0:n * 128].rearrange("p (j t) -> p j t", j=n)[:, :, 0:np_]
        copy_op(kb, ev.next(), dstT[:, g:g + n, dst_col0:dst_col0 + np_], tv, reads=[tp], writes=[dstT], partial=True)


class Rot:
    def __init__(self, items):
        self.items = items
        self.i = 0

    def next(self):
        it = self.items[self.i % len(self.items)]
        self.i += 1
        return it


L1_BLOCKS = ([(0, 512, "q"), (512, 512, "q"), (1024, 512, "kv"), (1536, 512, "kv"), (2048, 512, "kv"), (2560, 24, "gate")]
             + [(2584 + 512 * i, 512, "id") for i in range(2)] + [(3608 + 512 * i, 512, "f") for i in range(2)]
             + [(4632 + 512 * i, 512, "id") for i in range(2)]
             + [(5656 + 512 * i, 512, "silu") for i in range(2)] + [(6680 + 512 * i, 512, "sig") for i in range(8)])
NC1 = 10776


def build_L1():
    kb = KB()
    x = kb.din("x", [TPC, D])
    adav = kb.din("adav", [3, D])
    w = kb.din("w", [D, NC1])
    cosd = kb.din("cosd", [TPC, 64])
    sind = kb.din("sind", [TPC, 64])
    lbv = kb.din("lbv", [1, 1024])
    identd = kb.din("ident", [128, 128])
    z = kb.dout("z", [TPC, NC1])
    logf = kb.dout("logf", [TPC, 1024])

    ident = kb.sb("ident_s", [128, 128])
    kb.dma("sp", ident[:], identd[:], reads=[identd], writes=[ident])
    shb = kb.sb("shb", [128, D])
    gsc = kb.sb("gsc", [128, D])
    junk = kb.sb("junk", [128, D])
    gb = junk
    bcast_load(kb, "sp", shb, adav.t[0, :], adav)
    bcast_load(kb, "act", gsc, adav.t[1, :], adav)
    bcast_load(kb, "sp", gb, adav.t[2, :], adav)
    kb.op("dve", lambda e: e.scalar_tensor_tensor(out=gsc[:], in0=gsc[:], scalar=1.0, in1=gb[:], op0=ALU.add, op1=ALU.mult), reads=[gsc, gb], writes=[gsc])
    lbb = kb.sb("lbb", [128, 1024])
    oml = kb.sb("oml", [128, 1024])
    bcast_load(kb, "act", lbb, lbv.t[0, :], lbv)
    kb.op("dve", lambda e: e.tensor_scalar(out=oml[:], in0=lbb[:], scalar1=-1.0, scalar2=1.0, op0=ALU.mult, op1=ALU.add), reads=[lbb], writes=[oml])
    cs = kb.sb("cs", [128, NT, 64])
    sn = kb.sb("sn", [128, NT, 64])
    csq = kb.sb("csq", [128, NT, 64])
    snq = kb.sb("snq", [128, NT, 64])
    kb.dma("sp", cs[:], cosd.t.rearrange("(t p) f -> p t f", p=128), reads=[cosd], writes=[cs])
    kb.dma("act", sn[:], sind.t.rearrange("(t p) f -> p t f", p=128), reads=[sind], writes=[sn])
    SC = 128.0 ** -0.5
    kb.op("dve", lambda e: e.tensor_scalar(out=csq[:], in0=cs[:], scalar1=SC, scalar2=0.0, op0=ALU.mult, op1=ALU.add), reads=[cs], writes=[csq])
    kb.op("dve", lambda e: e.tensor_scalar(out=snq[:], in0=sn[:], scalar1=SC, scalar2=0.0, op0=ALU.mult, op1=ALU.add), reads=[sn], writes=[snq])
    epsb = kb.sb("epsb", [128, 1])
    kb.op("dve", lambda e: e.memset(epsb[:], 1e-6), writes=[epsb])

    xts = Rot([kb.sb("xt%d" % i, [128, D]) for i in range(2)])
    ss = kb.sb("ss", [128, 1])
    rstd = kb.sb("rstd", [128, 1])
    hs = Rot([kb.sb("h%d" % i, [128, D]) for i in range(1)])
    tps = Rot([kb.ps("tp%d" % i, [128, 512]) for i in range(2)])
    ev = Eng2(["act", "dve"])
    hT = [kb.sb("hT%d" % t, [128, 16, 128], F32R) for t in range(NT)]
    for tt in range(NT):
        xt = xts.next()
        h = hs.next()
        kb.dma("sp", xt[:], x[tt * 128:(tt + 1) * 128, :], reads=[x], writes=[xt])
        rms_rstd(kb, xt, 128, D, junk, ss, epsb, rstd)
        kb.op("dve", lambda e, xt=xt, h=h: e.scalar_tensor_tensor(out=h[:], in0=xt[:], scalar=rstd[:, 0:1], in1=gsc[:], op0=ALU.mult, op1=ALU.mult),
              reads=[xt, rstd, gsc], writes=[h])
        kb.op("pool", lambda e, h=h: e.tensor_tensor(out=h[:], in0=h[:], in1=shb[:], op=ALU.add), reads=[h, shb], writes=[h])
        transpose_into(kb, h, 128, 16, hT[tt], ident, tps, ev)

    wss = Rot([kb.sb("ws%d" % i, [128, 16, 512], F32R) for i in range(2)])
    pss = Rot([kb.ps("mm%d" % i, [128, 512]) for i in range(4)])
    zts = Rot([kb.sb("zt%d" % i, [128, 512]) for i in range(3)])
    lfs = Rot([kb.sb("lf%d" % i, [128, 512]) for i in range(2)])
    ta = kb.sb("ta", [128, 4, 64])
    tb = kb.sb("tb", [128, 4, 64])
    sg = kb.sb("sg", [128, 512])
    oq = Rot(["sp", "act"])

    def rope(p, zt, nh, c_t, s_t):
        pv = p[:, 0:nh * 128].rearrange("p (h t d) -> p h t d", h=nh, t=2)
        zv = zt[:, 0:nh * 128].rearrange("p (h t d) -> p h t d", h=nh, t=2)
        cb_ = c_t.unsqueeze(1).broadcast_to([128, nh, 64])
        sb_ = s_t.unsqueeze(1).broadcast_to([128, nh, 64])
        t1, t2 = pv[:, :, 0, :], pv[:, :, 1, :]
        kb.op("dve", lambda e: e.tensor_tensor(out=ta[:, 0:nh, :], in0=t1, in1=cb_, op=ALU.mult), reads=[p, cs, sn, csq, snq], writes=[ta])
        kb.op("dve", lambda e: e.tensor_tensor(out=tb[:, 0:nh, :], in0=t2, in1=sb_, op=ALU.mult), reads=[p, cs, sn, csq, snq], writes=[tb])
        kb.op("dve", lambda e: e.tensor_tensor(out=zv[:, :, 0, :], in0=ta[:, 0:nh, :], in1=tb[:, 0:nh, :], op=ALU.subtract), reads=[ta, tb], writes=[zt], partial=True)
        kb.op("dve", lambda e: e.tensor_tensor(out=ta[:, 0:nh, :], in0=t2, in1=cb_, op=ALU.mult), reads=[p, cs, sn, csq, snq], writes=[ta])
        kb.op("dve", lambda e: e.tensor_tensor(out=tb[:, 0:nh, :], in0=t1, in1=sb_, op=ALU.mult), reads=[p, cs, sn, csq, snq], writes=[tb])
        kb.op("dve", lambda e: e.tensor_tensor(out=zv[:, :, 1, :], in0=ta[:, 0:nh, :], in1=tb[:, 0:nh, :], op=ALU.add), reads=[ta, tb], writes=[zt], partial=True)

    for (c0, wd, kind) in L1_BLOCKS:
        ws_ = wss.next()
        kb.dma("pool", ws_[:, :, 0:wd], w.t[:, c0:c0 + wd].rearrange("(kc p) n -> p kc n", p=128), reads=[w], writes=[ws_])
        for tt in range(NT):
            p = pss.next()
            for kc in range(16):
                kb.op("pe", lambda e, kc=kc, p=p, tt=tt: e.matmul(p[:, 0:wd], hT[tt][:, kc, :], ws_[:, kc, 0:wd], start=(kc == 0), stop=(kc == 15)),
                      reads=[hT[tt], ws_], writes=[p], inc=(kc == 15))
            zt = zts.next()
            if kind == "q":
                rope(p, zt, 4, csq[:, tt, :], snq[:, tt, :])
            elif kind == "kv":
                rope(p, zt, 2, cs[:, tt, :], sn[:, tt, :])
                kb.op("act", lambda e, p=p, zt=zt: e.activation(out=zt[:, 256:512], in_=p[:, 256:512], func=AF.Copy), reads=[p], writes=[zt], partial=True)
            elif kind == "gate":
                kb.op("act", lambda e, p=p, zt=zt: e.activation(out=zt[:, 0:wd], in_=p[:, 0:wd], func=AF.Sigmoid), reads=[p], writes=[zt])
            elif kind == "id":
                kb.op("act", lambda e, p=p, zt=zt: e.activation(out=zt[:, 0:wd], in_=p[:, 0:wd], func=AF.Copy), reads=[p], writes=[zt])
            elif kind == "silu":
                kb.op("act", lambda e, p=p, zt=zt: e.activation(out=zt[:, 0:wd], in_=p[:, 0:wd], func=AF.Silu), reads=[p], writes=[zt])
            elif kind == "sig":
                kb.op("act", lambda e, p=p, zt=zt: e.activation(out=zt[:, 0:wd], in_=p[:, 0:wd], func=AF.Sigmoid), reads=[p], writes=[zt])
            else:
                fo = c0 - 3608
                lf = lfs.next()
                kb.op("act", lambda e, p=p: e.activation(out=sg[:], in_=p[:], func=AF.Sigmoid, scale=-1.0), reads=[p], writes=[sg])
                kb.op("dve", lambda e, zt=zt: e.tensor_tensor(out=zt[:], in0=sg[:], in1=oml[:, fo:fo + 512], op=ALU.mult), reads=[sg, oml], writes=[zt])
                kb.op("act", lambda e, p=p: e.activation(out=sg[:], in_=p[:], func=AF.Sigmoid), reads=[p], writes=[sg])
                kb.op("dve", lambda e: e.tensor_tensor(out=sg[:], in0=sg[:], in1=oml[:, fo:fo + 512], op=ALU.mult), reads=[sg, oml], writes=[sg])
                kb.op("dve", lambda e: e.tensor_tensor(out=sg[:], in0=sg[:], in1=lbb[:, fo:fo + 512], op=ALU.add), reads=[sg, lbb], writes=[sg])
                kb.op("act", lambda e, lf=lf: e.activation(out=lf[:], in_=sg[:], func=AF.Ln), reads=[sg], writes=[lf])
                kb.dma(oq.next(), logf[tt * 128:(tt + 1) * 128, fo:fo + 512], lf[:], reads=[lf], writes=[logf], partial=True, sbuf_side=lf)
            kb.dma(oq.next(), z[tt * 128:(tt + 1) * 128, c0:c0 + wd], zt[:, 0:wd], reads=[zt], writes=[z], partial=True, sbuf_side=zt)
    kb.finish()
    return kb


def run_L1(kb, xs, adav, w, cos, sin, lbv):
    ident = np.eye(128, dtype=np.float32)
    maps = []
    for j in range(8):
        sl = slice(j * TPC, (j + 1) * TPC)
        maps.append({"x": np.ascontiguousarray(xs[sl]), "adav": adav, "w": w, "cosd": np.ascontiguousarray(cos[sl]),
                     "sind": np.ascontiguousarray(sin[sl]), "lbv": lbv, "ident": ident})
    res = kb.run(maps).results
    z = np.concatenate([r["z"] for r in res], axis=0)
    logf = np.concatenate([r["logf"] for r in res], axis=0)
    return z, logf


NEG = -30000.0
S_ALL = 8192
NKT = S_ALL // 128


def slot_tile(s, j):
    return 8 * s + (j if s % 2 == 0 else 7 - j)


def build_L2(nslots=8, nkt_all=NKT, stop=99):
    kb = KB()
    qz = kb.din("qz", [1024, 1024])
    gz = kb.din("gz", [1024, 24])
    kva = kb.din("kva", [S_ALL, 1536])
    pekc = kb.din("pekc", [32, 128])
    wkc = kb.din("wkc", [4096, 128])
    pevc = kb.din("pevc", [32, 128])
    wvc = kb.din("wvc", [4096, 128])
    identd = kb.din("ident", [128, 128])
    kposd = kb.din("kposc", [128, 128])
    tposrd = kb.din("tposr", [1, 1024])
    tposcd = kb.din("tposc", [128, 8])
    cmpld = kb.din("cmpl", [1, 512])
    ovld = kb.din("ovl", [512, 128])
    addmd = kb.din("addm", [8, 128, 128])
    validd = kb.din("validm", [8, 128, 128])
    attn = kb.dout("attn", [1024, 1024])

    ident = kb.sb("ident_s", [128, 128])
    identr = kb.sb("identr", [128, 128], F32R)
    i4 = kb.sb("i4", [128, 4, 128], F32R)
    onesr = kb.sb("onesr", [1, 128], F32R)
    kposc = kb.sb("kposc_s", [128, 128])
    tposr = kb.sb("tposr_s", [128, 1024])
    tposc = kb.sb("tposc_s", [128, 8])
    cmpl = kb.sb("cmpl_s", [128, 512])
    gates = kb.sb("gates", [128, 8, 24])
    kb.dma("sp", ident[:], identd[:], reads=[identd], writes=[ident])
    kb.dma("sp", kposc[:], kposd[:], reads=[kposd], writes=[kposc])
    kb.dma("sp", tposc[:], tposcd[:], reads=[tposcd], writes=[tposc])
    bcast_load(kb, "sp", tposr, tposrd.t[0, :], tposrd)
    bcast_load(kb, "sp", cmpl, cmpld.t[0, :], cmpld)
    kb.dma("sp", gates[:], gz.t.rearrange("(s p) c -> p s c", p=128), reads=[gz], writes=[gates])
    kb.op("dve", lambda e: e.tensor_copy(out=identr[:], in_=ident[:]), reads=[ident], writes=[identr])
    for h in range(4):
        kb.op("dve", lambda e, h=h: e.tensor_copy(out=i4[:, h, :], in_=ident[:]), reads=[ident], writes=[i4], partial=True)
    onesf = kb.sb("onesf", [1, 128])
    kb.op("dve", lambda e: e.memset(onesf[:], 1.0), writes=[onesf])
    kb.op("dve", lambda e: e.tensor_copy(out=onesr[:], in_=onesf[:]), reads=[onesf], writes=[onesr])

    KA = kb.sb("KA", [128, 16 * 514], F32R)
    KBf = kb.sb("KBf", [128, 16 * 514], F32R)
    KA2 = KA[:].rearrange("p (r c) -> p r c", r=16)
    KB2 = KBf[:].rearrange("p (r c) -> p r c", r=16)
    Vs = kb.sb("Vs", [128, NKT, 130], BF16)
    Vw = kb.sb("Vw", [128, NKT, 130], BF16)
    WX = kb.sb("WX", [128, 2, 32, 128], F32R)
    Wk = Wv = WX
    WkA = WX[:, 0]
    WvA = WX[:, 1]
    kcmpT = kb.sb("kcmpT", [128, 512], F32R)
    vco = kb.sb("vco", [128, 4, 256], F32R)
    kb.op("pool", lambda e: e.memset(Vs[:, :, 128:130], 1.0), writes=[Vs])
    kb.op("pool", lambda e: e.memset(Vw[:, :, 128:130], 1.0), writes=[Vw])
    ovs = kb.sb("ovs", [128, 4, 128])
    kb.dma("sp", ovs[:], ovld.t.rearrange("(nt p) m -> p nt m", p=128), reads=[ovld], writes=[ovs])

    tps = Rot([kb.ps("tp%d" % i, [128, 512]) for i in range(2)])
    ev = Eng2(["act", "dve"])
    kvl = Rot([kb.sb("kvl%d" % i, [128, 4, 128]) for i in range(3)])
    lq = Rot(["sp"])

    def load_cw():
        kb.dma("pool", WkA, wkc.t.rearrange("(l d) o -> d l o", d=128), reads=[wkc], writes=[WX])
        kb.dma("pool", WvA, wvc.t.rearrange("(l d) o -> d l o", d=128), reads=[wvc], writes=[WX], partial=True)
    load_cw()
    pes = kb.sb("pes", [32, 2, 128])
    kb.dma("sp", pes[:, 0, :], pekc[:], reads=[pekc], writes=[pes], partial=True)
    kb.dma("sp", pes[:, 1, :], pevc[:], reads=[pevc], writes=[pes], partial=True)
    peT = kb.sb("peT", [128, 2, 34], F32R)
    kb.op("dve", lambda e: e.memset(peT[:].bitcast(F32), 0.0), writes=[peT])
    tp = tps.next()
    for i in range(2):
        kb.op("pe", lambda e, i=i: e.transpose(tp[:, i * 32:(i + 1) * 32], pes[:, i, :], ident[:32, :32]), reads=[pes, ident], writes=[tp], inc=(i == 1), partial=(i > 0))
    copy_op(kb, "dve", peT[:, :, 0:32], tp[:, 0:64].rearrange("p (i l) -> p i l", i=2), reads=[tp], writes=[peT], partial=True)
    if stop == 1:
        kb.finish()
        return kb

    kbias = kb.sb("kbias", [128, 1])
    vbrow = kb.sb("vbrow", [1, 128], F32R)
    tp = tps.next()
    for l in range(32):
        kb.op("pe", lambda e, l=l: e.matmul(tp[:, 0:2], WkA[:, l, :], peT[:, 0, l:l + 2], start=(l == 0), stop=(l == 31)), reads=[Wk, peT], writes=[tp], inc=(l == 31))
    copy_op(kb, "dve", kbias[:], tp[:, 0:1], reads=[tp], writes=[kbias])
    if stop == 2:
        kb.finish()
        return kb

    tp = tps.next()
    for l in range(32):
        kb.op("pe", lambda e, l=l: e.matmul(tp[0:2, 0:128], peT[:, 1, l:l + 2], WvA[:, l, :], start=(l == 0), stop=(l == 31)), reads=[Wv, peT], writes=[tp], inc=(l == 31))
    copy_op(kb, "dve", vbrow[:], tp[0:1, 0:128], reads=[tp], writes=[vbrow])
    if stop == 3:
        kb.finish()
        return kb


    osb = [kb.ps("os%d" % i, [128, 512]) for i in range(2)]
    owb = [kb.ps("ow%d" % i, [128, 512]) for i in range(2)]
    for b_ in osb + owb:
        b_.t = b_.t[:, 0:260].rearrange("p (a b) -> p a b", a=2)
    pcp = Rot([kb.ps("pc%d" % i, [128, 512]) for i in range(2)])
    qls = Rot([kb.sb("ql%d" % i, [128, 512]) for i in range(2)])
    QTs = Rot([kb.sb("QT%d" % i, [128, 4, 128], F32R) for i in range(2)])
    pcs = Rot([kb.sb("pcs%d" % i, [128, 512]) for i in range(2)])
    pTs = Rot([kb.sb("pT%d" % i, [128, 4, 128], F32R) for i in range(2)])
    PTs = Rot([kb.sb("PT%d" % i, [128, 512], BF16) for i in range(3)])
    mb4 = Rot([kb.sb("mb4_%d" % i, [128, 4, 128], F32R) for i in range(3)])
    mtmp = Rot([kb.sb("mtmp%d" % i, [128, 128]) for i in range(2)])
    mtmp2 = Rot([kb.sb("mtmq%d" % i, [128, 128]) for i in range(2)])
    accs = Rot([kb.sb("acc%d" % i, [128, 512]) for i in range(2)])
    cmask = kb.sb("cmask", [128, 512])
    addm = Rot([kb.sb("addm%d" % i, [128, 128]) for i in range(2)])
    validm = Rot([kb.sb("valm%d" % i, [128, 128]) for i in range(2)])
    impa = kb.sb("impa", [128, 128])
    impw = kb.sb("impw", [128, 128])
    impw2 = kb.sb("impw2", [128, 128])
    mx8 = kb.sb("mx8", [128, 8])
    negm = kb.sb("negm", [128, 128])
    negmx = WX
    negmxA = WX[:].rearrange("p a l o -> p (a l o)").rearrange("p (m k) -> p m k", k=64)
    sm = kb.sb("sm", [128, 8])
    oq = Rot(["sp"])

    for g in range(2):
        if g > 0:
            load_cw()
        for kt in range(nkt_all):
            t_ = kvl.next()
            src = kva.t[kt * 128:(kt + 1) * 128, 0:512].rearrange("p (i gg d) -> p i gg d", i=2, gg=2)[:, :, g, :]
            kb.dma(lq.next(), t_[:, 0:2, :], src, reads=[kva], writes=[t_])
            tp = tps.next()
            for i in range(2):
                kb.op("pe", lambda e, i=i, t_=t_, tp=tp: e.transpose(tp[:, i * 128:(i + 1) * 128], t_[:, i, :], ident[:]), reads=[t_, ident], writes=[tp], inc=(i == 1), partial=(i > 0))
            import os
            E = os.environ.get("EXP", "")
            if E != "noact":
                copy_op(kb, "dve" if E == "alldve" else "act", KA2[:, :, 8 * kt:8 * kt + 8], tp[:, 0:128].rearrange("p (cc r) -> p r cc", r=16), reads=[tp], writes=[KA], partial=True)
            if E != "nodve":
                copy_op(kb, "dve", KB2[:, :, 8 * kt:8 * kt + 8], tp[:, 128:256].rearrange("p (cc r) -> p r cc", r=16), reads=[tp], writes=[KBf], partial=True)
        import os
        if os.environ.get("DBG") != "1":
            kb.op("dve", lambda e: e.memset(KA2[:, :, 512:514].bitcast(F32), 0.0), writes=[KA], partial=True)
            kb.op("dve", lambda e: e.memset(KB2[:, :, 512:514].bitcast(F32), 0.0), writes=[KBf], partial=True)
        if stop == 4:
            kb.finish()
            return kb

        tp = tps.next()
        nb = (nkt_all * 128 - 32) // 16 + 1
        for l in range(32):
            rhs = KA2[:, l % 16, l // 16:l // 16 + 512]
            kb.op("pe", lambda e, l=l, rhs=rhs: e.matmul(tp[:, 0:512], WkA[:, l, :], rhs, start=(l == 0), stop=(l == 31)), reads=[Wk, KA], writes=[tp], inc=(l == 31))
        kb.op("act", lambda e: e.activation(out=kcmpT[:, 0:512], in_=tp[:, 0:512], func=AF.Identity, bias=kbias[:]), reads=[tp, kbias], writes=[kcmpT])
        if stop == 5:
            kb.finish()
            return kb

        for nt in range(4):
            n0 = nt * 128
            nn = 128
            tp = tps.next()
            for l in range(32):
                lhsT = KB2[:, l % 16, n0 + l // 16:n0 + l // 16 + 128]
                kb.op("pe", lambda e, l=l, lhsT=lhsT: e.matmul(tp[0:nn, 0:128], lhsT, WvA[:, l, :], start=(l == 0), stop=False), reads=[Wv, KBf], writes=[tp], inc=False)
            kb.op("pe", lambda e: e.matmul(tp[0:nn, 0:128], onesr[:, 0:nn], vbrow[:], start=False, stop=True), reads=[onesr, vbrow], writes=[tp])
            copy_op(kb, "act", vco[0:nn, nt, 0:128], tp[0:nn, 0:128], reads=[tp], writes=[vco], partial=(nt > 0))
        kb.op("dve", lambda e: e.tensor_copy(out=vco[:, :, 128:256], in_=ovs[:]), reads=[ovs], writes=[vco], partial=True)
        if stop == 6:
            kb.finish()
            return kb


        for kt in range(nkt_all):
            t_ = kvl.next()
            src = kva.t[kt * 128:(kt + 1) * 128, 512:1536].rearrange("p (i gg d) -> p i gg d", i=4, gg=2)[:, :, g, :]
            kb.dma(lq.next(), t_[:], src, reads=[kva], writes=[t_])
            tp = tps.next()
            for i in range(2):
                kb.op("pe", lambda e, i=i, t_=t_, tp=tp: e.transpose(tp[:, i * 128:(i + 1) * 128], t_[:, 2 * i, :], ident[:]), reads=[t_, ident], writes=[tp], inc=(i == 1), partial=(i > 0))
            copy_op(kb, "act", KA[:, kt * 128:(kt + 1) * 128], tp[:, 0:128], reads=[tp], writes=[KA], partial=True)
            copy_op(kb, "dve", KBf[:, kt * 128:(kt + 1) * 128], tp[:, 128:256], reads=[tp], writes=[KBf], partial=True)
            kb.op("pool", lambda e, t_=t_, kt=kt: e.tensor_copy(out=Vs[:, kt, 0:128], in_=t_[:, 1, :]), reads=[t_], writes=[Vs], partial=True)
            kb.op("pool", lambda e, t_=t_, kt=kt: e.tensor_copy(out=Vw[:, kt, 0:128], in_=t_[:, 3, :]), reads=[t_], writes=[Vw], partial=True)

        for s in range(nslots):
            nk_sel = min(8 * s + 8, nkt_all)
            k_lo = max(0, 8 * s - 4)
            ql = qls.next()
            QT = QTs.next()
            acc = accs.next()
            kb.dma("sp", ql[:], qz[s * 128:(s + 1) * 128, g * 512:(g + 1) * 512], reads=[qz], writes=[ql])
            tp = tps.next()
            for h in range(4):
                kb.op("pe", lambda e, h=h, tp=tp, ql=ql: e.transpose(tp[:, h * 128:(h + 1) * 128], ql[:, h * 128:(h + 1) * 128], ident[:]), reads=[ql, ident], writes=[tp], inc=(h == 3), partial=(h > 0))
            copy_op(kb, "act", QT[:].rearrange("p h q -> p (h q)"), tp[:], reads=[tp], writes=[QT])
            QTf = QT[:].rearrange("p h q -> p (h q)")
            if g == 0 or True:
                am = addm.next()
                vm = validm.next()
                kb.dma("sp", am[:], addmd[s], reads=[addmd], writes=[am])
                kb.dma("sp", vm[:], validd[s], reads=[validd], writes=[vm])
                kb.op("dve", lambda e: e.tensor_scalar(out=cmask[:], in0=cmpl[:], scalar1=tposc[:, s:s + 1], scalar2=0.0, op0=ALU.is_le, op1=ALU.add), reads=[cmpl, tposc], writes=[cmask])

            for h in range(4):
                hh = g * 4 + h
                sp_ = tps.next()
                kb.op("pe", lambda e, h=h, sp_=sp_: e.matmul(sp_[:], QT[:, h, :], kcmpT[:], start=True, stop=True), reads=[QT, kcmpT], writes=[sp_])
                kb.op("dve", lambda e, sp_=sp_: e.reduce_max(out=sm[:, 0:1], in_=sp_[:], axis=AX.X), reads=[sp_], writes=[sm], partial=True)
                kb.op("dve", lambda e: e.tensor_scalar(out=sm[:, 1:2], in0=sm[:, 0:1], scalar1=-1.0, scalar2=0.0, op0=ALU.mult, op1=ALU.add), reads=[sm], writes=[sm], partial=True)
                pc = pcs.next()
                kb.op("act", lambda e, sp_=sp_, pc=pc: e.activation(out=pc[:], in_=sp_[:], func=AF.Exp, bias=sm[:, 1:2]), reads=[sp_, sm], writes=[pc])
                kb.op("dve", lambda e, pc=pc: e.tensor_tensor(out=pc[:], in0=pc[:], in1=cmask[:], op=ALU.mult), reads=[pc, cmask], writes=[pc])
                kb.op("dve", lambda e, pc=pc: e.reduce_sum(out=sm[:, 2:3], in_=pc[:], axis=AX.X), reads=[pc], writes=[sm], partial=True)
                kb.op("dve", lambda e: e.tensor_scalar(out=sm[:, 2:3], in0=sm[:, 2:3], scalar1=1e-30, scalar2=0.0, op0=ALU.max, op1=ALU.add), reads=[sm], writes=[sm], partial=True)
                kb.op("dve", lambda e: e.reciprocal(out=sm[:, 3:4], in_=sm[:, 2:3]), reads=[sm], writes=[sm], partial=True)
                kb.op("dve", lambda e, pc=pc: e.tensor_scalar(out=pc[:], in0=pc[:], scalar1=sm[:, 3:4], scalar2=0.0, op0=ALU.mult, op1=ALU.add), reads=[pc, sm], writes=[pc])
                tpp = tps.next()
                for nt in range(4):
                    kb.op("pe", lambda e, nt=nt, tpp=tpp, pc=pc: e.transpose(tpp[:, nt * 128:(nt + 1) * 128], pc[:, nt * 128:(nt + 1) * 128], ident[:]), reads=[pc, ident], writes=[tpp], inc=(nt == 3), partial=(nt > 0))
                pT = pTs.next()
                copy_op(kb, "act", pT[:].rearrange("p a b -> p (a b)"), tpp[:], reads=[tpp], writes=[pT])
                pcb = pcp.next()
                for nt in range(4):
                    kb.op("pe", lambda e, nt=nt, pT=pT, pcb=pcb: e.matmul(pcb[:, 0:256], pT[:, nt, :], vco[:, nt, :], start=(nt == 0), stop=(nt == 3)), reads=[pT, vco], writes=[pcb], inc=(nt == 3))
                kb.op("dve", lambda e, h=h, pcb=pcb, hh=hh: e.tensor_scalar(out=acc[:, h * 128:(h + 1) * 128], in0=pcb[:, 0:128], scalar1=gates[:, s, hh * 3:hh * 3 + 1], scalar2=0.0, op0=ALU.mult, op1=ALU.add),
                      reads=[pcb, gates], writes=[acc], partial=(h > 0))
                if h == 0:
                    kb.op("dve", lambda e, pcb=pcb: e.tensor_tensor(out=impa[:], in0=pcb[:, 128:256], in1=am[:], op=ALU.add), reads=[pcb, am], writes=[impa])
                else:
                    kb.op("dve", lambda e, pcb=pcb: e.tensor_tensor(out=impa[:], in0=pcb[:, 128:256], in1=impa[:], op=ALU.add), reads=[pcb, impa], writes=[impa])
            kb.op("dve", lambda e: e.max(out=mx8[:], in_=impa[:]), reads=[impa], writes=[mx8])
            kb.op("dve", lambda e: e.match_replace(out=impw[:], in_to_replace=mx8[:], in_values=impa[:], imm_value=-3.0e38), reads=[mx8, impa], writes=[impw])
            kb.op("dve", lambda e: e.max(out=mx8[:], in_=impw[:]), reads=[impw], writes=[mx8])
            kb.op("dve", lambda e: e.match_replace(out=impw2[:], in_to_replace=mx8[:], in_values=impw[:], imm_value=-3.0e38), reads=[mx8, impw], writes=[impw2])
            kb.op("dve", lambda e: e.tensor_tensor(out=impw[:], in0=impa[:], in1=impw2[:], op=ALU.not_equal), reads=[impa, impw2], writes=[impw])
            kb.op("dve", lambda e: e.tensor_tensor(out=impw[:], in0=impw[:], in1=vm[:], op=ALU.mult), reads=[impw, vm], writes=[impw])
            nm_ = 2 * min(8 * s + 8, nkt_all)
            kb.op("dve", lambda e: e.tensor_scalar(out=negmxA[:, 0:nm_, :], in0=impw[:, 0:nm_].unsqueeze(2).broadcast_to([128, nm_, 64]), scalar1=-1.0, scalar2=-NEG, op0=ALU.add, op1=ALU.mult),
                  reads=[impw], writes=[negmx])

            def kbranch(KT, V, ob, kts, sel):
                nkt_ = len(kts)
                for ii, kt in enumerate(kts):
                    sp_ = tps.next()
                    kb.op("pe", lambda e, kt=kt, sp_=sp_: e.matmul(sp_[:], KT[:, kt * 128:(kt + 1) * 128], QTf, start=True, stop=False), reads=[KT, QT], writes=[sp_], inc=False)
                    posmask = (not sel) or (kt >= 8 * s)
                    if sel:
                        lhsT = negmxA[:, 2 * kt:2 * kt + 2, :].rearrange("p a b -> p (a b)")
                        kb.op("pe", lambda e, sp_=sp_, lhsT=lhsT: e.matmul(sp_[:], lhsT, i4[:].rearrange("p h q -> p (h q)"), start=False, stop=(not posmask)),
                              reads=[negmx, i4], writes=[sp_], inc=(not posmask), partial=True)
                    if posmask:
                        m4 = mb4.next()
                        ma = mtmp.next()
                        kb.op("dve", lambda e, kt=kt, ma=ma: e.tensor_scalar(out=ma[:], in0=tposr[:, s * 128:(s + 1) * 128], scalar1=kposc[:, kt:kt + 1], scalar2=0.0, op0=ALU.is_ge, op1=ALU.add),
                              reads=[tposr, kposc], writes=[ma])
                        if not sel:
                            mb_ = mtmp2.next()
                            kb.op("dve", lambda e, kt=kt, mb_=mb_: e.tensor_scalar(out=mb_[:], in0=tposr[:, s * 128:(s + 1) * 128], scalar1=kposc[:, 64 + kt:64 + kt + 1], scalar2=0.0, op0=ALU.is_lt, op1=ALU.add),
                                  reads=[tposr, kposc], writes=[mb_])
                            kb.op("dve", lambda e, ma=ma, mb_=mb_: e.tensor_tensor(out=ma[:], in0=ma[:], in1=mb_[:], op=ALU.mult), reads=[ma, mb_], writes=[ma])
                        kb.op("dve", lambda e, ma=ma, m4=m4: e.tensor_scalar(out=m4[:], in0=ma[:].unsqueeze(1).broadcast_to([128, 4, 128]), scalar1=-1.0, scalar2=-NEG, op0=ALU.add, op1=ALU.mult),
                              reads=[ma], writes=[m4])
                        kb.op("pe", lambda e, sp_=sp_, m4=m4: e.matmul(sp_[:], identr[:], m4[:].rearrange("p h q -> p (h q)"), start=False, stop=True), reads=[identr, m4], writes=[sp_], partial=True)
                    PT = PTs.next()
                    kb.op("act", lambda e, sp_=sp_, PT=PT: e.activation(out=PT[:], in_=sp_[:], func=AF.Exp), reads=[sp_], writes=[PT])
                    for h in range(4):
                        kb.op("pe", lambda e, h=h, PT=PT, kt=kt, ii=ii: e.matmul(ob[h // 2][:, h % 2, :], PT[:, h * 128:(h + 1) * 128], V[:, kt, :], start=(ii == 0 and h % 2 == 0), stop=(ii == nkt_ - 1), skip_group_check=True),
                              reads=[PT, V], writes=[ob[h // 2]], inc=(h == 3), partial=(not (ii == 0 and h % 2 == 0)))
                for h in range(4):
                    hh = g * 4 + h
                    o_ = ob[h // 2]
                    gcol = hh * 3 + (1 if sel else 2)
                    kb.op("dve", lambda e, o_=o_, h=h: e.tensor_scalar(out=sm[:, 4:5], in0=o_[:, h % 2, 128:129], scalar1=1e-30, scalar2=0.0, op0=ALU.max, op1=ALU.add), reads=[o_], writes=[sm], partial=True)
                    kb.op("dve", lambda e: e.reciprocal(out=sm[:, 5:6], in_=sm[:, 4:5]), reads=[sm], writes=[sm], partial=True)
                    kb.op("dve", lambda e, gcol=gcol: e.tensor_tensor(out=sm[:, 6:7], in0=sm[:, 5:6], in1=gates[:, s, gcol:gcol + 1], op=ALU.mult), reads=[sm, gates], writes=[sm], partial=True)
                    kb.op("dve", lambda e, o_=o_, h=h: e.scalar_tensor_tensor(out=acc[:, h * 128:(h + 1) * 128], in0=o_[:, h % 2, 0:128], scalar=sm[:, 6:7], in1=acc[:, h * 128:(h + 1) * 128], op0=ALU.mult, op1=ALU.add),
                          reads=[o_, sm, acc], writes=[acc], partial=True)

            kbranch(KBf, Vw, owb, list(range(k_lo, nk_sel)), sel=False)
            kbranch(KA, Vs, osb, list(range(0, nk_sel)), sel=True)
            kb.dma(oq.next(), attn[s * 128:(s + 1) * 128, g * 512:(g + 1) * 512], acc[:], reads=[acc], writes=[attn], partial=True, sbuf_side=acc)
    kb.finish()
    return kb


def l2_tables(j):
    tiles = [slot_tile(s, j) for s in range(8)]
    tpos = np.concatenate([np.arange(t * 128, (t + 1) * 128) for t in tiles]).astype(np.float32)
    tposr = tpos.reshape(1, 1024)
    tposc = np.ascontiguousarray(tpos.reshape(8, 128).T)
    addm = np.zeros((8, 128, 128), np.float32)
    validm = np.zeros((8, 128, 128), np.float32)
    m = np.arange(128)[None, :]
    for s, t in enumerate(tiles):
        tp = (t * 128 + np.arange(128))[:, None]
        cur = tp // 64
        a = np.zeros((128, 128), np.float32)
        a[np.broadcast_to(m > cur, (128, 128))] = -1e30
        a[np.broadcast_to(m == cur - 1, (128, 128))] = 1e30
        a[np.broadcast_to(m == cur, (128, 128))] = 2e30
        a[:, 0] = 3e30
        addm[s] = a
        validm[s] = (m <= cur).astype(np.float32)
    return tposr, tposc, addm, validm


def l2_consts():
    kpos = (np.arange(64)[None, :] * 128 + np.arange(128)[:, None]).astype(np.float32)
    kposc = np.concatenate([kpos, kpos + 512], axis=1)
    n = np.arange(512)
    cmpl = (16 * n + 31).astype(np.float32).reshape(1, 512)
    cmpl[0, 511] = 1e9
    cs = n[:, None] * 16
    ss = np.arange(128)[None, :] * 64
    ov = np.maximum(np.minimum(cs + 32, ss + 64) - np.maximum(cs, ss), 0).astype(np.float32) / 32
    ov[511] = 0
    return np.ascontiguousarray(kposc), cmpl, np.ascontiguousarray(ov)


def run_L2(kb, z, wl):
    ident = np.eye(128, dtype=np.float32)
    kposc, cmpl, ov = l2_consts()
    kva = np.ascontiguousarray(z[:, 1024:2560])
    maps = []
    rows_all = []
    for j in range(8):
        tiles = [slot_tile(s, j) for s in range(8)]
        rows = np.concatenate([np.arange(t * 128, (t + 1) * 128) for t in tiles])
        rows_all.append(rows)
        tposr, tposc, addm, validm = l2_tables(j)
        maps.append({"qz": np.ascontiguousarray(z[rows, 0:1024]), "gz": np.ascontiguousarray(z[rows, 2560:2584]), "kva": kva,
                     "pekc": wl["pe_kc"], "wkc": wl["w_kc"], "pevc": wl["pe_vc"], "wvc": wl["w_vc"], "ident": ident,
                     "kposc": kposc, "tposr": tposr, "tposc": tposc, "cmpl": cmpl, "ovl": ov, "addm": addm, "validm": validm})
    res = kb.run(maps).results
    attn = np.zeros((S_ALL, 1024), np.float32)
    for j in range(8):
        attn[rows_all[j]] = res[j]["attn"]
    return attn


def l3_consts():
    t = np.arange(128)
    ch = t // 32
    mid = ch * 32 + 15
    same = ch[:, None] == ch[None, :]
    L = ((t[:, None] <= t[None, :]) & same).astype(np.float32)
    M = ((t[:, None] <= mid[None, :]) & same).astype(np.float32)
    Lm = L - M
    Mid = np.zeros((128, 4), np.float32)
    for c in range(4):
        Mid[:, c] = ((ch == c) & (t <= c * 32 + 15)).astype(np.float32)
    lmx = np.concatenate([Lm, Mid], axis=1)
    rowmask = (ch[:, None] == np.arange(4)[None, :]).astype(np.float32)
    return np.ascontiguousarray(lmx), np.ascontiguousarray(L), np.ascontiguousarray(rowmask)


def build_L3(ntiles=NKT):
    kb = KB()
    qkgv = kb.din("qkgv", [S_ALL, 4, 128])
    lmxd = kb.din("lmx", [128, 132])
    maskd = kb.din("mask32", [128, 128])
    rowmd = kb.din("rowmask", [128, 4])
    identd = kb.din("ident", [128, 128])
    od = kb.dout("o", [S_ALL, 128])

    ident = kb.sb("ident_s", [128, 128])
    lmx = kb.sb("lmx_s", [128, 132])
    mask32 = kb.sb("mask_s", [128, 128])
    rowm = kb.sb("rowm_s", [128, 4])
    kb.dma("sp", ident[:], identd[:], reads=[identd], writes=[ident])
    kb.dma("sp", lmx[:], lmxd[:], reads=[lmxd], writes=[lmx])
    kb.dma("sp", mask32[:], maskd[:], reads=[maskd], writes=[mask32])
    kb.dma("sp", rowm[:], rowmd[:], reads=[rowmd], writes=[rowm])

    ins_ = Rot([kb.sb("in%d" % i, [128, 4, 128]) for i in range(3)])
    pA = Rot([kb.ps("pA%d" % i, [128, 512]) for i in range(2)])
    pB = Rot([kb.ps("pB%d" % i, [128, 512]) for i in range(2)])
    pC = kb.ps("pC", [128, 512])
    pD = kb.ps("pD", [128, 512])
    pE = kb.ps("pE", [128, 512])
    ebs = Rot([kb.sb("eb%d" % i, [128, 3, 128]) for i in range(2)])
    ems = Rot([kb.sb("em%d" % i, [128, 3, 4]) for i in range(2)])
    qes = Rot([kb.sb("qe%d" % i, [128, 128], F32R) for i in range(2)])
    kes = Rot([kb.sb("ke%d" % i, [128, 128], F32R) for i in range(2)])
    qss = [Rot([kb.sb("qs%d_%d" % (c, i), [128, 128], F32R) for i in range(2)]) for c in range(4)]
    kms = [Rot([kb.sb("km%d_%d" % (c, i), [128, 128], F32R) for i in range(2)]) for c in range(4)]
    vrs = Rot([kb.sb("vr%d" % i, [128, 128], F32R) for i in range(2)])
    scs = Rot([kb.sb("sc%d" % i, [128, 128], F32R) for i in range(2)])
    oss = Rot([kb.sb("os%d" % i, [128, 128]) for i in range(2)])
    for c in range(4):
        for b_ in qss[c].items:
            kb.op("pool", lambda e, b_=b_: e.memset(b_[:].bitcast(F32), 0.0), writes=[b_])
    Sf = Rot([kb.sb("Sf%d" % i, [128, 128]) for i in range(3)])
    Sr = Rot([kb.sb("Sr%d" % i, [128, 128], F32R) for i in range(3)])
    tmp = kb.sb("stmp", [128, 128])
    s_prev_f = Sf.next()
    s_prev_r = Sr.next()
    kb.op("dve", lambda e: e.memset(s_prev_f[:], 0.0), writes=[s_prev_f])
    kb.op("dve", lambda e: e.memset(s_prev_r[:].bitcast(F32), 0.0), writes=[s_prev_r])

    for ti in range(ntiles):
        it = ins_.next()
        kb.dma("sp", it[:], qkgv[ti * 128:(ti + 1) * 128], reads=[qkgv], writes=[it])
        a = pA.next()
        b = pB.next()
        kb.op("pe", lambda e: e.matmul(a[:, 0:132], it[:, 2, :], lmx[:], start=True, stop=True), reads=[it, lmx], writes=[a], inc=False)
        kb.op("pe", lambda e: e.matmul(a[:, 256:384], lmx[:, 0:128], it[:, 2, :], start=True, stop=True), reads=[it, lmx], writes=[a], partial=True)
        kb.op("pe", lambda e: e.transpose(b[:, 0:128], it[:, 0, :], ident[:]), reads=[it, ident], writes=[b], inc=False)
        kb.op("pe", lambda e: e.transpose(b[:, 128:256], it[:, 1, :], ident[:]), reads=[it, ident], writes=[b], partial=True)
        eb = ebs.next()
        em = ems.next()
        kb.op("act", lambda e: e.activation(out=eb[:, 0, :], in_=a[:, 0:128], func=AF.Exp), reads=[a], writes=[eb])
        kb.op("act", lambda e: e.activation(out=em[:, 0, :], in_=a[:, 128:132], func=AF.Exp), reads=[a], writes=[em])
        kb.op("act", lambda e: e.activation(out=eb[:, 1, :], in_=a[:, 0:128], func=AF.Exp, scale=-1.0), reads=[a], writes=[eb], partial=True)
        kb.op("act", lambda e: e.activation(out=eb[:, 2, :], in_=a[:, 256:384], func=AF.Exp, scale=-1.0), reads=[a], writes=[eb], partial=True)
        if os.environ.get('L3STOP') == '1':
            kb.finish(); return kb

        kb.op("dve", lambda e: e.tensor_copy(out=em[:, 1, :], in_=eb[:, 0, 31:128:32]), reads=[eb], writes=[em], partial=True)
        kb.op("dve", lambda e: e.tensor_tensor(out=em[:, 2, :], in0=em[:, 0, :], in1=em[:, 1, :], op=ALU.mult), reads=[em], writes=[em], partial=True)
        if os.environ.get('L3STOP') == '2':
            kb.finish(); return kb

        qe = qes.next(); ke = kes.next(); vr = vrs.next(); sc = scs.next()
        kb.op("dve", lambda e: e.tensor_tensor(out=qe[:], in0=b[:, 0:128], in1=eb[:, 0, :], op=ALU.mult), reads=[b, eb], writes=[qe])
        kb.op("dve", lambda e: e.tensor_tensor(out=ke[:], in0=b[:, 128:256], in1=eb[:, 1, :], op=ALU.mult), reads=[b, eb], writes=[ke])
        kb.op("pool", lambda e: e.tensor_copy(out=vr[:], in_=it[:, 3, :]), reads=[it], writes=[vr])
        if os.environ.get('L3STOP') == '3':
            kb.finish(); return kb

        qs = [qss[c].next() for c in range(4)]
        km = [kms[c].next() for c in range(4)]
        for c in range(4):
            kb.op("dve", lambda e, c=c: e.scalar_tensor_tensor(out=km[c][:], in0=it[:, 1, :], scalar=rowm[:, c:c + 1], in1=eb[:, 2, :], op0=ALU.mult, op1=ALU.mult),
                  reads=[it, rowm, eb], writes=[km[c]])
            kb.op("dve", lambda e, c=c: e.tensor_scalar(out=qs[c][:, 32 * c:32 * c + 32], in0=qe[:, 32 * c:32 * c + 32], scalar1=em[:, 0, c:c + 1], scalar2=0.0, op0=ALU.mult, op1=ALU.add),
                  reads=[qe, em], writes=[qs[c]], partial=True)
        kb.op("pe", lambda e: e.matmul(pC[:, 0:128], ke[:], qe[:], start=True, stop=True), reads=[ke, qe], writes=[pC])
        kb.op("dve", lambda e: e.tensor_tensor(out=sc[:], in0=pC[:, 0:128], in1=mask32[:], op=ALU.mult), reads=[pC, mask32], writes=[sc])
        if os.environ.get('L3STOP') == '4':
            kb.finish(); return kb

        for c in range(4):
            kb.op("pe", lambda e, c=c: e.matmul(pD[:, 128 * c:128 * c + 128], km[c][:], vr[:], start=True, stop=True), reads=[km[c], vr], writes=[pD], inc=(c == 3), partial=(c > 0))
        kb.op("pe", lambda e: e.matmul(pE[:, 0:128], sc[:], vr[:], start=True, stop=False), reads=[sc, vr], writes=[pE], inc=False)
        for c in range(4):
            kb.op("pe", lambda e, c=c: e.matmul(pE[:, 0:128], qs[c][:], s_prev_r[:], start=False, stop=(c == 3)), reads=[qs[c], s_prev_r], writes=[pE], partial=True)
            sf = Sf.next(); sr = Sr.next()
            kb.op("dve", lambda e, c=c: e.tensor_scalar(out=tmp[:], in0=s_prev_f[:], scalar1=em[:, 2, c:c + 1], scalar2=0.0, op0=ALU.mult, op1=ALU.add), reads=[s_prev_f, em], writes=[tmp])
            kb.op("dve", lambda e, c=c, sf=sf: e.scalar_tensor_tensor(out=sf[:], in0=pD[:, 128 * c:128 * c + 128], scalar=em[:, 1, c:c + 1], in1=tmp[:], op0=ALU.mult, op1=ALU.add), reads=[pD, em, tmp], writes=[sf])
            kb.op("act", lambda e, sf=sf, sr=sr: e.activation(out=sr[:], in_=sf[:], func=AF.Copy), reads=[sf], writes=[sr])
            s_prev_f, s_prev_r = sf, sr
        os_ = oss.next()
        kb.op("act", lambda e: e.activation(out=os_[:], in_=pE[:, 0:128], func=AF.Copy), reads=[pE], writes=[os_])
        kb.dma("sp", od[ti * 128:(ti + 1) * 128, :], os_[:], reads=[os_], writes=[od], partial=True, sbuf_side=os_)
    kb.finish()
    return kb


def run_L3(kb, z, logf):
    ident = np.eye(128, dtype=np.float32)
    lmx, L, rowmask = l3_consts()
    maps = []
    for j in range(8):
        c = slice(j * 128, (j + 1) * 128)
        qkgv = np.stack([z[:, 2584:3608][:, c], z[:, 3608:4632][:, c], logf[:, c], z[:, 4632:5656][:, c]], axis=1)
        maps.append({"qkgv": np.ascontiguousarray(qkgv), "lmx": lmx, "mask32": L, "rowmask": rowmask, "ident": ident})
    res = kb.run(maps).results
    return np.concatenate([r["o"] for r in res], axis=1)


def build_L4a():
    kb = KB()
    TH = 512
    x = kb.din("x", [TPC, D])
    attnT = kb.din("attnT", [1024, TPC])
    hg0T = kb.din("hg0T", [1024, TPC])
    ogT = kb.din("ogT", [1024, TPC])
    mgT = kb.din("mgT", [4096, TPC])
    ghg = kb.din("ghg", [128, 8])
    vec = kb.din("vec", [2, D])
    wba = kb.din("wba", [1024, D])
    wbh = kb.din("wbh", [1024, D])
    wo = kb.din("wo", [D, D])
    x1 = kb.dout("x1", [TPC, D])

    U0 = kb.sb("U0", [128, 4096], F32R)
    U1 = kb.sb("U1", [128, 4096], F32R)
    aT = U0[:].rearrange("p (k t) -> p k t", k=8)
    hT = U1[:].rearrange("p (k t) -> p k t", k=8)
    MT = kb.sb("MT", [128, 16, TH], F32R)
    ghs = kb.sb("ghs", [128, 8])
    gpg = kb.sb("gpg", [128, D])
    gtb = kb.sb("gtb", [128, D])
    epsb = kb.sb("epsb", [128, 1])
    onesf = kb.sb("onesf", [128, 128])
    onesr = kb.sb("onesr", [128, 128], F32R)
    kb.op("dve", lambda e: e.memset(epsb[:], 1e-6), writes=[epsb])
    kb.op("dve", lambda e: e.memset(onesf[:], 1.0), writes=[onesf])
    kb.op("dve", lambda e: e.tensor_copy(out=onesr[:], in_=onesf[:]), reads=[onesf], writes=[onesr])
    kb.dma("sp", ghs[:], ghg[:], reads=[ghg], writes=[ghs])
    bcast_load(kb, "sp", gtb, vec.t[0, :], vec)
    bcast_load(kb, "sp", gpg, vec.t[1, :], vec)
    kb.op("dve", lambda e: e.tensor_tensor(out=gpg[:], in0=gpg[:], in1=gtb[:], op=ALU.mult), reads=[gpg, gtb], writes=[gpg])
    pss = Rot([kb.ps("ps%d" % i, [128, 512]) for i in range(6)])
    h0s = Rot([kb.sb("h0_%d" % i, [128, TH]) for i in range(2)])
    ogs = Rot([kb.sb("og_%d" % i, [128, TH]) for i in range(2)])
    sqs = Rot([kb.sb("sq_%d" % i, [128, TH], F32R) for i in range(2)])
    rs = Rot([kb.sb("r_%d" % i, [128, TH]) for i in range(2)])
    was = Rot([kb.sb("wa_%d" % i, [128, 8, 128], F32R) for i in range(2)])
    wbs = Rot([kb.sb("wb_%d" % i, [128, 8, 128], F32R) for i in range(2)])
    gas = Rot([kb.sb("ga_%d" % i, [128, TH]) for i in range(2)])
    gbs = Rot([kb.sb("gb_%d" % i, [128, TH]) for i in range(2)])
    t1s = Rot([kb.sb("t1_%d" % i, [128, TH]) for i in range(2)])
    t2s = Rot([kb.sb("t2_%d" % i, [128, TH]) for i in range(2)])
    wos = Rot([kb.sb("wo_%d" % i, [128, 16, 256], F32R) for i in range(2)])
    xts = Rot([kb.sb("xt%d" % i, [128, D]) for i in range(2)])
    ots = Rot([kb.sb("ot%d" % i, [128, D]) for i in range(2)])
    ss = kb.sb("ss", [128, 1]); rstd = kb.sb("rstd", [128, 1])
    ev = Eng2(["act", "dve"])
    Y = [U0[:].rearrange("p (t c) -> p t c", t=2), U1[:].rearrange("p (t c) -> p t c", t=2)]
    YB = [U0, U1]

    for half in range(TPC // TH):
        tsl = slice(half * TH, (half + 1) * TH)
        kb.dma("pool", aT, attnT.t[:, tsl].rearrange("(k p) t -> p k t", p=128), reads=[attnT], writes=[U0])
        for kc in range(8):
            h0 = h0s.next(); og = ogs.next(); sq = sqs.next()
            kb.dma("sp", h0[:], hg0T[kc * 128:(kc + 1) * 128, tsl], reads=[hg0T], writes=[h0])
            kb.dma("sp", og[:], ogT[kc * 128:(kc + 1) * 128, tsl], reads=[ogT], writes=[og])
            kb.op("act", lambda e: e.activation(out=sq[:], in_=h0[:], func=AF.Square), reads=[h0], writes=[sq])
            p = pss.next()
            kb.op("pe", lambda e: e.matmul(p[:], onesr[:], sq[:], start=True, stop=True), reads=[onesr, sq], writes=[p])
            r = rs.next()
            kb.op("act", lambda e: e.activation(out=r[:], in_=p[:], func=AF.Sqrt, bias=epsb[:], scale=1.0 / 128), reads=[p, epsb], writes=[r])
            kb.op("dve", lambda e: e.reciprocal(out=r[:], in_=r[:]), reads=[r], writes=[r])
            kb.op("dve", lambda e: e.scalar_tensor_tensor(out=r[:], in0=h0[:], scalar=ghs[:, kc:kc + 1], in1=r[:], op0=ALU.mult, op1=ALU.mult), reads=[h0, ghs, r], writes=[r])
            kb.op("pool", lambda e: e.tensor_tensor(out=hT[:, kc, :], in0=r[:], in1=og[:], op=ALU.mult), reads=[r, og], writes=[U1], partial=(kc > 0))
        for cc in range(16):
            wa = was.next(); wb_ = wbs.next(); ga = gas.next(); gb_ = gbs.next()
            kb.dma("pool", wa[:], wba.t[:, cc * 128:(cc + 1) * 128].rearrange("(k p) n -> p k n", p=128), reads=[wba], writes=[wa])
            kb.dma("pool", wb_[:], wbh.t[:, cc * 128:(cc + 1) * 128].rearrange("(k p) n -> p k n", p=128), reads=[wbh], writes=[wb_])
            kb.dma("sp", ga[:], mgT[cc * 128:(cc + 1) * 128, tsl], reads=[mgT], writes=[ga])
            kb.dma("sp", gb_[:], mgT[2048 + cc * 128:2048 + (cc + 1) * 128, tsl], reads=[mgT], writes=[gb_])
            pa = pss.next()
            for kc in range(8):
                kb.op("pe", lambda e, kc=kc: e.matmul(pa[:], wa[:, kc, :], aT[:, kc, :], start=(kc == 0), stop=(kc == 7)), reads=[wa, U0], writes=[pa], inc=(kc == 7))
            pb = pss.next()
            for kc in range(8):
                kb.op("pe", lambda e, kc=kc: e.matmul(pb[:], wb_[:, kc, :], hT[:, kc, :], start=(kc == 0), stop=(kc == 7)), reads=[wb_, U1], writes=[pb], inc=(kc == 7))
            t1 = t1s.next(); t2 = t2s.next()
            kb.op("dve", lambda e: e.tensor_tensor(out=t1[:], in0=pa[:], in1=ga[:], op=ALU.mult), reads=[pa, ga], writes=[t1])
            kb.op("dve", lambda e: e.tensor_tensor(out=t2[:], in0=pb[:], in1=gb_[:], op=ALU.mult), reads=[pb, gb_], writes=[t2])
            kb.op("pool", lambda e: e.tensor_tensor(out=MT[:, cc, :], in0=t1[:], in1=t2[:], op=ALU.add), reads=[t1, t2], writes=[MT], partial=(cc > 0))
        for cb in range(8):
            w_ = wos.next()
            kb.dma("pool", w_[:], wo.t[:, cb * 256:(cb + 1) * 256].rearrange("(k p) n -> p k n", p=128), reads=[wo], writes=[w_])
            for tt in range(4):
                p = pss.next()
                for kc in range(16):
                    kb.op("pe", lambda e, kc=kc: e.matmul(p[:, 0:256], MT[:, kc, tt * 128:(tt + 1) * 128], w_[:, kc, :], start=(kc == 0), stop=(kc == 15)), reads=[MT, w_], writes=[p], inc=(kc == 15))
                copy_op(kb, ev.next(), Y[tt // 2][:, tt % 2, cb * 256:(cb + 1) * 256], p[:, 0:256], reads=[p], writes=[YB[tt // 2]], partial=not (cb == 0 and tt % 2 == 0))
        for tt in range(4):
            xt = xts.next(); ot = ots.next()
            yv = Y[tt // 2][:, tt % 2, :]
            r0 = half * TH + tt * 128
            kb.dma("sp", xt[:], x[r0:r0 + 128, :], reads=[x], writes=[xt])
            kb.op("act", lambda e: e.activation(out=ot[:], in_=yv, func=AF.Square, accum_out=ss[:]), reads=[YB[tt // 2]], writes=[ot, ss])
            kb.op("act", lambda e: e.activation(out=rstd[:], in_=ss[:], func=AF.Sqrt, bias=epsb[:], scale=1.0 / D), reads=[ss, epsb], writes=[rstd])
            kb.op("dve", lambda e: e.reciprocal(out=rstd[:], in_=rstd[:]), reads=[rstd], writes=[rstd])
            kb.op("dve", lambda e: e.scalar_tensor_tensor(out=ot[:], in0=yv, scalar=rstd[:, 0:1], in1=gpg[:], op0=ALU.mult, op1=ALU.mult), reads=[YB[tt // 2], rstd, gpg], writes=[ot])
            kb.op("pool", lambda e: e.tensor_tensor(out=ot[:], in0=ot[:], in1=xt[:], op=ALU.add), reads=[ot, xt], writes=[ot])
            kb.dma("sp", x1[r0:r0 + 128, :], ot[:], reads=[ot], writes=[x1], partial=True, sbuf_side=ot)
    kb.finish()
    return kb


def run_L4a(kb, xs, attn, hg0, z, ada_l, wl):
    maps = []
    ghg = np.ascontiguousarray(wl["g_hg_norm"].reshape(8, 128).T)
    vec = np.ascontiguousarray(np.stack([ada_l[4096:6144], wl["g_post_mix"]]))
    for j in range(8):
        sl = slice(j * TPC, (j + 1) * TPC)
        maps.append({"x": np.ascontiguousarray(xs[sl]), "attnT": np.ascontiguousarray(attn[sl].T), "hg0T": np.ascontiguousarray(hg0[sl].T),
                     "ogT": np.ascontiguousarray(z[sl, 5656:6680].T), "mgT": np.ascontiguousarray(z[sl, 6680:10776].T),
                     "ghg": ghg, "vec": vec, "wba": wl["w_br_attn"], "wbh": wl["w_br_hgrn"], "wo": wl["w_out"]})
    res = kb.run(maps).results
    return np.concatenate([r["x1"] for r in res], axis=0)


DFF = 5632


def build_L4b():
    kb = KB()
    TH = 512
    x1 = kb.din("x1", [TPC, D])
    xh = kb.din("xh", [2, D])
    hmaskd = kb.din("hmask", [128, 1])
    vec = kb.din("vec", [5, D])
    cwd = kb.din("cw", [128, 88, 4])
    wup = kb.din("wup", [D, 2 * DFF])
    wdn = kb.din("wdn", [DFF, D])
    identd = kb.din("ident", [128, 128])
    x2 = kb.dout("x2", [TPC, D])

    ident = kb.sb("ident_s", [128, 128])
    hmask = kb.sb("hmask_s", [128, 1])
    cw = kb.sb("cw_s", [128, 88, 4])
    epsb = kb.sb("epsb", [128, 1])
    shb = kb.sb("shb", [128, D])
    gsc = kb.sb("gsc", [128, D])
    gpg = kb.sb("gpg", [128, D])
    kb.dma("sp", ident[:], identd[:], reads=[identd], writes=[ident])
    kb.dma("sp", hmask[:], hmaskd[:], reads=[hmaskd], writes=[hmask])
    kb.dma("sp", cw[:], cwd[:], reads=[cwd], writes=[cw])
    kb.op("dve", lambda e: e.memset(epsb[:], 1e-6), writes=[epsb])
    xts = Rot([kb.sb("xt%d" % i, [128, D]) for i in range(2)])
    hb = kb.sb("hb", [128, D])
    tmpb = xts.items[1]
    bcast_load(kb, "sp", shb, vec.t[0, :], vec)
    bcast_load(kb, "sp", gsc, vec.t[1, :], vec)
    bcast_load(kb, "sp", tmpb, vec.t[2, :], vec)
    kb.op("dve", lambda e: e.scalar_tensor_tensor(out=gsc[:], in0=gsc[:], scalar=1.0, in1=tmpb[:], op0=ALU.add, op1=ALU.mult), reads=[gsc, tmpb], writes=[gsc])
    bcast_load(kb, "sp", gpg, vec.t[3, :], vec)
    bcast_load(kb, "sp", tmpb, vec.t[4, :], vec)
    kb.op("dve", lambda e: e.tensor_tensor(out=gpg[:], in0=gpg[:], in1=tmpb[:], op=ALU.mult), reads=[gpg, tmpb], writes=[gpg])

    H2T = kb.sb("H2T", [128, 16, TH], F32R)
    Y = H2T[:].rearrange("p k t -> p (k t)").rearrange("p (t c) -> p t c", t=4)
    H2Th = kb.sb("H2Th", [128, 16, 2], F32R)
    aT = kb.sb("aT", [128, 44, TH], BF16)
    wgs = Rot([kb.sb("wg%d" % i, [128, 16, 128], F32R) for i in range(2)])
    wvs = Rot([kb.sb("wv%d" % i, [128, 16, 128], F32R) for i in range(2)])
    wds = Rot([kb.sb("wd%d" % i, [128, 44, 128], BF16) for i in range(2)])
    ugs = Rot([kb.sb("ug%d" % i, [128, 514]) for i in range(2)])
    uvs = Rot([kb.sb("uv%d" % i, [128, 514]) for i in range(2)])
    tgs = Rot([kb.sb("tg%d" % i, [128, TH]) for i in range(2)])
    tvs = Rot([kb.sb("tv%d" % i, [128, TH]) for i in range(2)])
    sgs = Rot([kb.sb("sg%d" % i, [128, TH]) for i in range(2)])
    ss = kb.sb("ss", [128, 1]); rstd = kb.sb("rstd", [128, 1])
    tps = Rot([kb.ps("tp%d" % i, [128, 512]) for i in range(2)])
    pss = Rot([kb.ps("mm%d" % i, [128, 512]) for i in range(4)])
    phs = Rot([kb.ps("ph%d" % i, [128, 512]) for i in range(2)])
    ev = Eng2(["act", "dve"])

    def norm_mod_T(xt, np_, dstT, col0):
        rms_rstd(kb, xt, np_, D, hb, ss, epsb, rstd)
        kb.op("dve", lambda e: e.scalar_tensor_tensor(out=hb[:np_, :], in0=xt[:np_, :], scalar=rstd[:np_, 0:1], in1=gsc[:np_, :], op0=ALU.mult, op1=ALU.mult),
              reads=[xt, rstd, gsc], writes=[hb])
        kb.op("pool", lambda e: e.tensor_tensor(out=hb[:np_, :], in0=hb[:np_, :], in1=shb[:np_, :], op=ALU.add), reads=[hb, shb], writes=[hb])
        transpose_into(kb, hb, np_, 16, dstT, ident, tps, ev, dst_col0=col0)

    for half in range(TPC // TH):
        T0 = half * TH
        xt = xts.next()
        if half == 0:
            kb.dma("sp", xt[0:2, :], xh[:], reads=[xh], writes=[xt])
        else:
            kb.dma("sp", xt[0:2, :], x1[T0 - 2:T0, :], reads=[x1], writes=[xt])
        norm_mod_T(xt, 2, H2Th, 0)
        for tt in range(4):
            xt = xts.next()
            kb.dma("sp", xt[:], x1[T0 + tt * 128:T0 + (tt + 1) * 128, :], reads=[x1], writes=[xt])
            norm_mod_T(xt, 128, H2T, tt * 128)
        for c in range(44):
            wg = wgs.next(); wv = wvs.next()
            kb.dma("pool", wg[:], wup.t[:, c * 128:(c + 1) * 128].rearrange("(k p) n -> p k n", p=128), reads=[wup], writes=[wg])
            kb.dma("pool", wv[:], wup.t[:, DFF + c * 128:DFF + (c + 1) * 128].rearrange("(k p) n -> p k n", p=128), reads=[wup], writes=[wv])
            outs = []
            for (w_, ubr, tbr, ch) in ((wg, ugs, tgs, c), (wv, uvs, tvs, 44 + c)):
                ub = ubr.next(); tb = tbr.next()
                p = pss.next()
                for kc in range(16):
                    kb.op("pe", lambda e, kc=kc: e.matmul(p[:, 0:TH], w_[:, kc, :], H2T[:, kc, :], start=(kc == 0), stop=(kc == 15)), reads=[w_, H2T], writes=[p], inc=(kc == 15))
                ph = phs.next()
                for kc in range(16):
                    kb.op("pe", lambda e, kc=kc: e.matmul(ph[:, 0:2], w_[:, kc, :], H2Th[:, kc, :], start=(kc == 0), stop=(kc == 15)), reads=[w_, H2Th], writes=[ph], inc=(kc == 15))
                kb.op("act", lambda e: e.activation(out=ub[:, 2:2 + TH], in_=p[:, 0:TH], func=AF.Copy), reads=[p], writes=[ub])
                if half == 0:
                    kb.op("dve", lambda e: e.tensor_scalar(out=ub[:, 0:2], in0=ph[:, 0:2], scalar1=hmask[:, 0:1], scalar2=0.0, op0=ALU.mult, op1=ALU.add), reads=[ph, hmask], writes=[ub], partial=True)
                else:
                    kb.op("dve", lambda e: e.tensor_copy(out=ub[:, 0:2], in_=ph[:, 0:2]), reads=[ph], writes=[ub], partial=True)
                kb.op("act", lambda e: e.activation(out=tb[:], in_=ub[:, 2:2 + TH], func=AF.Identity, bias=cw[:, ch, 3:4], scale=cw[:, ch, 2:3]), reads=[ub, cw], writes=[tb])
                kb.op("dve", lambda e: e.scalar_tensor_tensor(out=tb[:], in0=ub[:, 1:1 + TH], scalar=cw[:, ch, 1:2], in1=tb[:], op0=ALU.mult, op1=ALU.add), reads=[ub, cw, tb], writes=[tb])
                kb.op("dve", lambda e: e.scalar_tensor_tensor(out=tb[:], in0=ub[:, 0:TH], scalar=cw[:, ch, 0:1], in1=tb[:], op0=ALU.mult, op1=ALU.add), reads=[ub, cw, tb], writes=[tb])
                outs.append(tb)
            sg = sgs.next()
            kb.op("act", lambda e: e.activation(out=sg[:], in_=outs[0][:], func=AF.Silu), reads=[outs[0]], writes=[sg])
            kb.op("pool", lambda e: e.tensor_tensor(out=aT[:, c, :], in0=sg[:], in1=outs[1][:], op=ALU.mult), reads=[sg, outs[1]], writes=[aT], partial=(c > 0))
        for cb in range(16):
            wd = wds.next()
            kb.dma("pool", wd[:], wdn.t[:, cb * 128:(cb + 1) * 128].rearrange("(k p) n -> p k n", p=128), reads=[wdn], writes=[wd])
            for tt in range(4):
                p = pss.next()
                for kc in range(44):
                    kb.op("pe", lambda e, kc=kc: e.matmul(p[:, 0:128], aT[:, kc, tt * 128:(tt + 1) * 128], wd[:, kc, :], start=(kc == 0), stop=(kc == 43)), reads=[aT, wd], writes=[p], inc=(kc == 43))
                copy_op(kb, ev.next(), Y[:, tt, cb * 128:(cb + 1) * 128], p[:, 0:128], reads=[p], writes=[H2T], partial=not (cb == 0 and tt == 0))
        for tt in range(4):
            xt = xts.next()
            yv = Y[:, tt, :]
            r0 = T0 + tt * 128
            kb.dma("sp", xt[:], x1[r0:r0 + 128, :], reads=[x1], writes=[xt])
            kb.op("act", lambda e: e.activation(out=hb[:], in_=yv, func=AF.Square, accum_out=ss[:]), reads=[H2T], writes=[hb, ss])
            kb.op("act", lambda e: e.activation(out=rstd[:], in_=ss[:], func=AF.Sqrt, bias=epsb[:], scale=1.0 / D), reads=[ss, epsb], writes=[rstd])
            kb.op("dve", lambda e: e.reciprocal(out=rstd[:], in_=rstd[:]), reads=[rstd], writes=[rstd])
            kb.op("dve", lambda e: e.scalar_tensor_tensor(out=hb[:], in0=yv, scalar=rstd[:, 0:1], in1=gpg[:], op0=ALU.mult, op1=ALU.mult), reads=[H2T, rstd, gpg], writes=[hb])
            kb.op("pool", lambda e: e.tensor_tensor(out=xt[:], in0=hb[:], in1=xt[:], op=ALU.add), reads=[hb, xt], writes=[xt])
            kb.dma("sp", x2[r0:r0 + 128, :], xt[:], reads=[xt], writes=[x2], partial=True, sbuf_side=xt)
    kb.finish()
    return kb


def run_L4b(kb, x1, ada_l, wl):
    ident = np.eye(128, dtype=np.float32)
    vec = np.ascontiguousarray(np.stack([ada_l[6144:8192], ada_l[8192:10240], wl["g_pre_ffn"], ada_l[10240:12288], wl["g_post_ffn"]]))
    cwf = np.concatenate([wl["conv_w"], wl["conv_b"][None, :]], axis=0)
    cw = np.ascontiguousarray(cwf.reshape(4, 88, 128).transpose(2, 1, 0))
    maps = []
    for j in range(8):
        sl = slice(j * TPC, (j + 1) * TPC)
        xh = np.ascontiguousarray(x1[j * TPC - 2:j * TPC]) if j > 0 else np.zeros((2, D), np.float32)
        hm = np.full((128, 1), 1.0 if j > 0 else 0.0, np.float32)
        maps.append({"x1": np.ascontiguousarray(x1[sl]), "xh": xh, "hmask": hm, "vec": vec, "cw": cw,
                     "wup": wl["w_up"], "wdn": wl["w_down"], "ident": ident})
    res = kb.run(maps).results
    return np.concatenate([r["x2"] for r in res], axis=0)


_PROGS = {}


def _prog(name, fn):
    if name not in _PROGS:
        _PROGS[name] = fn()
    return _PROGS[name]


def kernel(x, c, positions, w_ada, b_ada, g_pre_mix, w_in, pe_kc, w_kc, pe_vc, w_vc, lb_logits, g_hg_norm,
           w_br_attn, w_br_hgrn, w_out, g_post_mix, g_pre_ffn, w_up, conv_w, conv_b, w_down, g_post_ffn):
    f = lambda a: np.ascontiguousarray(np.asarray(a, dtype=np.float32))
    inp = {"c": f(c), "w_ada": f(w_ada), "b_ada": f(b_ada), "lb_logits": f(lb_logits), "positions": np.asarray(positions)}
    ada, lb, cos, sin = run_L0(inp)
    xs = f(x)[0]
    for l in range(4):
        adav = np.ascontiguousarray(np.stack([ada[l, 0:2048], ada[l, 2048:4096], f(g_pre_mix[l])]))
        z, logf = run_L1(_prog("L1", build_L1), xs, adav, f(w_in[l]), cos, sin, np.ascontiguousarray(lb[l:l + 1]))
        wl = {"pe_kc": f(pe_kc[l]), "w_kc": f(w_kc[l]), "pe_vc": f(pe_vc[l]), "w_vc": f(w_vc[l])}
        attn = run_L2(_prog("L2", build_L2), z, wl)
        hg0 = run_L3(_prog("L3", build_L3), z, logf)
        wl = {"g_hg_norm": f(g_hg_norm[l]), "g_post_mix": f(g_post_mix[l]), "w_br_attn": f(w_br_attn[l]),
              "w_br_hgrn": f(w_br_hgrn[l]), "w_out": f(w_out[l])}
        x1 = run_L4a(_prog("L4a", build_L4a), xs, attn, hg0, z, ada[l], wl)
        del z, logf, attn, hg0
        wl = {"g_pre_ffn": f(g_pre_ffn[l]), "g_post_ffn": f(g_post_ffn[l]), "conv_w": f(conv_w[l]), "conv_b": f(conv_b[l]),
              "w_up": f(w_up[l]), "w_down": f(w_down[l])}
        xs = run_L4b(_prog("L4b", build_L4b), x1, ada[l], wl)
    return xs.reshape(1, S_ALL, D).astype(np.float32)
```

```python
import numpy as np
from contextlib import ExitStack
import concourse.bass as bass
import concourse.mybir as mybir
from concourse.bass_utils import run_bass_kernel_spmd

F32 = mybir.dt.float32
F32R = mybir.dt.float32r
BF16 = mybir.dt.bfloat16
I32 = mybir.dt.int32
AF = mybir.ActivationFunctionType
ALU = mybir.AluOpType
AX = mybir.AxisListType


class Buf:
    def __init__(self, kb, t, name):
        self.kb = kb
        self.t = t
        self.name = name
        self.w = {}
        self.r = {}
        self.dsem = None
        self.dval = 0
        self.excl = False

    def __getitem__(self, k):
        return self.t[k]

    def ap(self):
        return self.t[:]


class KB:
    def __init__(self):
        self.nc = bass.Bass("TRN2", target_bir_lowering=False)
        nc = self.nc
        self.es = ExitStack()
        self.eng = {"pe": nc.tensor, "act": nc.scalar, "dve": nc.vector, "pool": nc.gpsimd, "sp": nc.sync}
        self.sem = {e: self.es.enter_context(nc.semaphore("s_" + e)) for e in self.eng}
        self.cnt = {e: 0 for e in self.eng}
        self.seen = {e: {} for e in self.eng}
        self.outs = []
        self.nsem = 0

    def sb(self, name, shape, dt=F32):
        t = self.es.enter_context(self.nc.sbuf_tensor(name, list(shape), dt))
        return Buf(self, t, name)

    def ps(self, name, shape, dt=F32):
        t = self.es.enter_context(self.nc.psum_tensor(name, list(shape), dt))
        b = Buf(self, t, name)
        b.excl = True
        return b

    def view(self, t, name):
        return Buf(self, t, name)

    def din(self, name, shape, dt=F32):
        t = self.nc.dram_tensor(name, list(shape), dt, kind="ExternalInput").ap()
        return Buf(self, t, name)

    def dout(self, name, shape, dt=F32):
        t = self.nc.dram_tensor(name, list(shape), dt, kind="ExternalOutput").ap()
        b = Buf(self, t, name)
        self.outs.append(b)
        return b

    def dsem_of(self, b):
        if b.dsem is None:
            self.nsem += 1
            b.dsem = self.es.enter_context(self.nc.semaphore("d%d" % self.nsem))
        return b.dsem

    def _wait(self, e, key, sem, val):
        if self.seen[e].get(key, 0) >= val:
            return
        self.eng[e].wait_ge(sem, val)
        self.seen[e][key] = val

    def _deps(self, e, own, reads, writes):
        for b in reads:
            for key, (sem, val) in b.w.items():
                self._wait(e, key, sem, val)
            if b.excl:
                for key, (sem, val) in b.r.items():
                    if key != own:
                        self._wait(e, key, sem, val)
        for b in writes:
            for key, (sem, val) in list(b.w.items()) + list(b.r.items()):
                if key == own:
                    continue
                self._wait(e, key, sem, val)

    def _mark(self, key, tok, reads, writes, partial):
        for b in reads:
            b.r[key] = tok
        for b in writes:
            if partial:
                b.w[key] = tok
            else:
                b.w = {key: tok}
                b.r = {}

    def op(self, e, emit, reads=(), writes=(), inc=True, partial=False):
        self._deps(e, e, reads, writes)
        ins = emit(self.eng[e])
        if inc:
            self.cnt[e] += 1
            ins.then_inc(self.sem[e], 1)
            tokv = self.cnt[e]
        else:
            tokv = self.cnt[e] + 1
        self._mark(e, (self.sem[e], tokv), reads, writes, partial)
        return ins

    def dma(self, q, out_ap, in_ap, reads=(), writes=(), partial=False, sbuf_side=None):
        sb = sbuf_side if sbuf_side is not None else (writes[0] if writes else reads[0])
        sem = self.dsem_of(sb)
        key = ("d", id(sb))
        self._deps(q, key, reads, writes)
        ins = self.eng[q].dma_start(out=out_ap, in_=in_ap)
        sb.dval += 16
        ins.then_inc(sem, 16)
        self._mark(key, (sem, sb.dval), reads, writes, partial)
        return ins

    def finish(self):
        for b in self.outs:
            for key, (sem, val) in b.w.items():
                self._wait("sp", key, sem, val)
        for e in ("pe", "act", "dve", "pool"):
            if self.cnt[e]:
                self._wait("sp", e, self.sem[e], self.cnt[e])

    def run(self, in_maps, trace=False):
        import os
        if os.environ.get("KTRACE") == "1":
            trace = True
        res = run_bass_kernel_spmd(self.nc, in_maps, core_ids=list(range(len(in_maps))), trace=trace)
        if trace:
            print("EXEC_TIME_NS", getattr(res, "exec_time_ns", None))
        return res

import math
import os
import numpy as np

D = 2048
TPC = 1024
NT = TPC // 128
PI = math.pi


def build_L0():
    kb = KB()
    c2 = kb.din("c2", [128, 16])
    wada = kb.din("wada", [4, 2048, 1536])
    bada = kb.din("bada", [1, 4 * 1536])
    lbl = kb.din("lbl", [128, 4, 8])
    pos = kb.din("pos", [128, 8], I32)
    invf = kb.din("invf", [128, 64])
    ada_o = kb.dout("ada_o", [1, 4 * 1536])
    lb_o = kb.dout("lb_o", [128, 4, 8])
    cos_o = kb.dout("cos_o", [1024, 64])
    sin_o = kb.dout("sin_o", [1024, 64])

    cs = kb.sb("cs", [128, 16])
    cact = kb.sb("cact", [128, 16])
    bs = kb.sb("bs", [1, 4 * 1536])
    ws = [kb.sb("ws%d" % i, [128, 16, 512]) for i in range(2)]
    pss = [kb.ps("ps%d" % i, [1, 512]) for i in range(2)]
    ao = kb.sb("ao", [1, 4 * 1536])
    kb.dma("sp", cs[:], c2[:], reads=[c2], writes=[cs])
    kb.dma("sp", bs[:], bada[:], reads=[bada], writes=[bs])
    kb.op("act", lambda e: e.activation(out=cact[:], in_=cs[:], func=AF.Silu), reads=[cs], writes=[cact])
    i = 0
    for l in range(4):
        for cb in range(3):
            w = ws[i % 2]
            p = pss[i % 2]
            src = wada[l, :, cb * 512:(cb + 1) * 512].rearrange("(kc p) n -> p kc n", p=128)
            kb.dma("sp" if i % 2 == 0 else "act", w[:], src, reads=[wada], writes=[w])
            for kc in range(16):
                kb.op("pe", lambda e, kc=kc, w=w, p=p: e.matmul(p[:], cact[:, kc:kc + 1], w[:, kc, :], start=(kc == 0), stop=(kc == 15)),
                      reads=[cact, w], writes=[p], inc=(kc == 15))
            o0 = l * 1536 + cb * 512
            kb.op("dve", lambda e, p=p, o0=o0: e.tensor_tensor(out=ao[:, o0:o0 + 512], in0=p[:], in1=bs[:, o0:o0 + 512], op=ALU.add),
                  reads=[p, bs], writes=[ao], partial=True)
            i += 1
    kb.dma("sp", ada_o[:], ao[:], reads=[ao], writes=[ada_o], sbuf_side=ao)

    ll = kb.sb("ll", [128, 4, 8])
    le = kb.sb("le", [128, 4, 8])
    lsum = kb.sb("lsum", [128, 8])
    lo = kb.sb("lo", [128, 4, 8])
    kb.dma("sp", ll[:], lbl[:], reads=[lbl], writes=[ll])
    kb.op("act", lambda e: e.activation(out=le[:], in_=ll[:], func=AF.Exp), reads=[ll], writes=[le])
    kb.op("dve", lambda e: e.tensor_tensor(out=lsum[:], in0=le[:, 0, :], in1=le[:, 1, :], op=ALU.add), reads=[le], writes=[lsum])
    kb.op("dve", lambda e: e.tensor_tensor(out=lsum[:], in0=lsum[:], in1=le[:, 2, :], op=ALU.add), reads=[le, lsum], writes=[lsum])
    kb.op("dve", lambda e: e.tensor_tensor(out=lsum[:], in0=lsum[:], in1=le[:, 3, :], op=ALU.add), reads=[le, lsum], writes=[lsum])
    kb.op("dve", lambda e: e.reciprocal(out=lsum[:], in_=lsum[:]), reads=[lsum], writes=[lsum])
    kb.op("dve", lambda e: e.memset(lo[:, 0, :], 0.0), writes=[lo])
    for l in range(1, 4):
        kb.op("dve", lambda e, l=l: e.tensor_tensor(out=le[:, l, :], in0=le[:, l, :], in1=lsum[:], op=ALU.mult), reads=[le, lsum], writes=[le])
        kb.op("dve", lambda e, l=l: e.tensor_tensor(out=lo[:, l, :], in0=lo[:, l - 1, :], in1=le[:, l, :], op=ALU.add), reads=[le, lo], writes=[lo])
    kb.dma("sp", lb_o[:], lo[:], reads=[lo], writes=[lb_o], sbuf_side=lo)

    pi_ = kb.sb("pi_", [128, 8], I32)
    pf = kb.sb("pf", [128, 8])
    ivf = kb.sb("ivf", [128, 64])
    ang = kb.sb("ang", [128, 8, 64])
    m1 = kb.sb("m1", [128, 8, 64])
    m2 = kb.sb("m2", [128, 8, 64])
    so = kb.sb("so", [128, 8, 64])
    co = kb.sb("co", [128, 8, 64])
    negpi = kb.sb("negpi", [128, 1])
    kb.op("dve", lambda e: e.memset(negpi[:], -PI), writes=[negpi])
    kb.dma("sp", pi_[:], pos[:], reads=[pos], writes=[pi_])
    kb.dma("sp", ivf[:], invf[:], reads=[invf], writes=[ivf])
    kb.op("dve", lambda e: e.tensor_copy(out=pf[:], in_=pi_[:]), reads=[pi_], writes=[pf])
    for t in range(8):
        kb.op("dve", lambda e, t=t: e.tensor_scalar(out=ang[:, t, :], in0=ivf[:], scalar1=pf[:, t:t + 1], scalar2=0.0, op0=ALU.mult, op1=ALU.add),
              reads=[ivf, pf], writes=[ang], partial=True)
    C1 = 6.28125
    C2 = 2 * PI - C1
    PIC = 3.1415925
    ki = kb.sb("ki", [128, 8, 64], I32)
    kf = kb.sb("kf", [128, 8, 64])
    wr = kb.sb("wr", [128, 8, 64])
    kb.op("dve", lambda e: e.tensor_scalar(out=m1[:], in0=ang[:], scalar1=1.0 / (2 * PI), scalar2=0.0, op0=ALU.mult, op1=ALU.add), reads=[ang], writes=[m1])
    kb.op("dve", lambda e: e.tensor_copy(out=ki[:], in_=m1[:]), reads=[m1], writes=[ki])
    kb.op("dve", lambda e: e.tensor_copy(out=kf[:], in_=ki[:]), reads=[ki], writes=[kf])
    kb.op("dve", lambda e: e.scalar_tensor_tensor(out=m1[:], in0=kf[:], scalar=-C1, in1=ang[:], op0=ALU.mult, op1=ALU.add), reads=[kf, ang], writes=[m1])
    kb.op("dve", lambda e: e.scalar_tensor_tensor(out=m1[:], in0=kf[:], scalar=-C2, in1=m1[:], op0=ALU.mult, op1=ALU.add), reads=[kf, m1], writes=[m1])

    def wrap(buf):
        kb.op("dve", lambda e: e.tensor_scalar(out=wr[:], in0=buf[:], scalar1=PI, scalar2=-2 * PI, op0=ALU.is_gt, op1=ALU.mult), reads=[buf], writes=[wr])
        kb.op("dve", lambda e: e.tensor_tensor(out=buf[:], in0=buf[:], in1=wr[:], op=ALU.add), reads=[buf, wr], writes=[buf])
        kb.op("dve", lambda e: e.tensor_scalar(out=wr[:], in0=buf[:], scalar1=-PI, scalar2=2 * PI, op0=ALU.is_lt, op1=ALU.mult), reads=[buf], writes=[wr])
        kb.op("dve", lambda e: e.tensor_tensor(out=buf[:], in0=buf[:], in1=wr[:], op=ALU.add), reads=[buf, wr], writes=[buf])
        kb.op("dve", lambda e: e.tensor_scalar(out=buf[:], in0=buf[:], scalar1=-PIC, scalar2=PIC, op0=ALU.max, op1=ALU.min), reads=[buf], writes=[buf])
    wrap(m1)
    kb.op("dve", lambda e: e.tensor_scalar(out=m2[:], in0=m1[:], scalar1=0.5 * PI, scalar2=0.0, op0=ALU.add, op1=ALU.add), reads=[m1], writes=[m2])
    wrap(m2)
    kb.op("act", lambda e: e.activation(out=so[:], in_=m1[:], func=AF.Sin), reads=[m1], writes=[so])
    kb.op("act", lambda e: e.activation(out=co[:], in_=m2[:], func=AF.Sin), reads=[m2], writes=[co])
    kb.dma("sp", sin_o.t.rearrange("(t p) f -> p t f", p=128), so[:], reads=[so], writes=[sin_o], sbuf_side=so)
    kb.dma("sp", cos_o.t.rearrange("(t p) f -> p t f", p=128), co[:], reads=[co], writes=[cos_o], sbuf_side=co)
    kb.finish()
    return kb


def run_L0(inp):
    kb = build_L0()
    c = np.asarray(inp["c"], np.float32)
    inv_freq = (1.0 / (10000.0 ** (np.arange(0, 128, 2, dtype=np.float32) / np.float32(128)))).astype(np.float32)
    maps = []
    for j in range(8):
        maps.append({
            "c2": np.ascontiguousarray(c.reshape(16, 128).T),
            "wada": np.ascontiguousarray(inp["w_ada"][:, :, j * 1536:(j + 1) * 1536]),
            "bada": np.ascontiguousarray(inp["b_ada"][:, j * 1536:(j + 1) * 1536]).reshape(1, -1),
            "lbl": np.ascontiguousarray(inp["lb_logits"].reshape(4, 8, 128).transpose(2, 0, 1)),
            "pos": np.ascontiguousarray(inp["positions"][0, j * 1024:(j + 1) * 1024].reshape(8, 128).T.astype(np.int32)),
            "invf": np.ascontiguousarray(np.broadcast_to(inv_freq[None, :], (128, 64))),
        })
    res = kb.run(maps).results
    ada = np.concatenate([r["ada_o"].reshape(4, 1536) for r in res], axis=1)
    lb = res[0]["lb_o"].transpose(1, 2, 0).reshape(4, 1024)
    cos = np.concatenate([r["cos_o"] for r in res], axis=0)
    sin = np.concatenate([r["sin_o"] for r in res], axis=0)
    return ada, lb, cos, sin


class Eng2:
    def __init__(self, names):
        self.names = names
        self.i = 0

    def next(self):
        n = self.names[self.i % len(self.names)]
        self.i += 1
        return n


def copy_op(kb, eng, out_ap, in_ap, reads, writes, partial=False):
    if eng == "act":
        return kb.op("act", lambda e: e.activation(out=out_ap, in_=in_ap, func=AF.Copy), reads=reads, writes=writes, partial=partial)
    return kb.op(eng, lambda e: e.tensor_copy(out=out_ap, in_=in_ap), reads=reads, writes=writes, partial=partial)


def bcast_load(kb, q, dst, src_row_ap, src_buf):
    kb.dma(q, dst[:], src_row_ap.partition_broadcast(128), reads=[src_buf], writes=[dst])


def rms_rstd(kb, xt, np_, ncols, junk, ss, epsb, rstd):
    kb.op("act", lambda e: e.activation(out=junk[:np_, :ncols], in_=xt[:np_, :ncols], func=AF.Square, accum_out=ss[:np_, :]),
          reads=[xt], writes=[junk, ss])
    kb.op("act", lambda e: e.activation(out=rstd[:np_, :], in_=ss[:np_, :], func=AF.Sqrt, bias=epsb[:np_, :], scale=1.0 / ncols),
          reads=[ss, epsb], writes=[rstd])
    kb.op("dve", lambda e: e.reciprocal(out=rstd[:np_, :], in_=rstd[:np_, :]), reads=[rstd], writes=[rstd])


def transpose_into(kb, src, np_, nchunks, dstT, ident, tps, ev, dst_col0=0):
    for g in range(0, nchunks, 4):
        n = min(4, nchunks - g)
        tp = tps.next()
        for j in range(n):
            kc = g + j
            kb.op("pe", lambda e, kc=kc, j=j, tp=tp: e.transpose(tp[:, j * 128:j * 128 + np_], src[:np_, kc * 128:(kc + 1) * 128], ident[:np_, :np_]),
                  reads=[src, ident], writes=[tp], inc=(j == n - 1), partial=(j > 0))
        tv = tp[:, 0:n * 128].rearrange("p (j t) -> p j t", j=n)[:, :, 0:np_]
        copy_op(kb, ev.next(), dstT[:, g:g + n, dst_col0:dst_col0 + np_], tv, reads=[tp], writes=[dstT], partial=True)


class Rot:
    def __init__(self, items):
        self.items = items
        self.i = 0

    def next(self):
        it = self.items[self.i % len(self.items)]
        self.i += 1
        return it


L1_BLOCKS = ([(0, 512, "q"), (512, 512, "q"), (1024, 512, "kv"), (1536, 512, "kv"), (2048, 512, "kv"), (2560, 24, "gate")]
             + [(2584 + 512 * i, 512, "id") for i in range(2)] + [(3608 + 512 * i, 512, "f") for i in range(2)]
             + [(4632 + 512 * i, 512, "id") for i in range(2)]
             + [(5656 + 512 * i, 512, "silu") for i in range(2)] + [(6680 + 512 * i, 512, "sig") for i in range(8)])
NC1 = 10776


def build_L1():
    kb = KB()
    x = kb.din("x", [TPC, D])
    adav = kb.din("adav", [3, D])
    w = kb.din("w", [D, NC1])
    cosd = kb.din("cosd", [TPC, 64])
    sind = kb.din("sind", [TPC, 64])
    lbv = kb.din("lbv", [1, 1024])
    identd = kb.din("ident", [128, 128])
    z = kb.dout("z", [TPC, NC1])
    logf = kb.dout("logf", [TPC, 1024])

    ident = kb.sb("ident_s", [128, 128])
    kb.dma("sp", ident[:], identd[:], reads=[identd], writes=[ident])
    shb = kb.sb("shb", [128, D])
    gsc = kb.sb("gsc", [128, D])
    junk = kb.sb("junk", [128, D])
    gb = junk
    bcast_load(kb, "sp", shb, adav.t[0, :], adav)
    bcast_load(kb, "act", gsc, adav.t[1, :], adav)
    bcast_load(kb, "sp", gb, adav.t[2, :], adav)
    kb.op("dve", lambda e: e.scalar_tensor_tensor(out=gsc[:], in0=gsc[:], scalar=1.0, in1=gb[:], op0=ALU.add, op1=ALU.mult), reads=[gsc, gb], writes=[gsc])
    lbb = kb.sb("lbb", [128, 1024])
    oml = kb.sb("oml", [128, 1024])
    bcast_load(kb, "act", lbb, lbv.t[0, :], lbv)
    kb.op("dve", lambda e: e.tensor_scalar(out=oml[:], in0=lbb[:], scalar1=-1.0, scalar2=1.0, op0=ALU.mult, op1=ALU.add), reads=[lbb], writes=[oml])
    cs = kb.sb("cs", [128, NT, 64])
    sn = kb.sb("sn", [128, NT, 64])
    csq = kb.sb("csq", [128, NT, 64])
    snq = kb.sb("snq", [128, NT, 64])
    kb.dma("sp", cs[:], cosd.t.rearrange("(t p) f -> p t f", p=128), reads=[cosd], writes=[cs])
    kb.dma("act", sn[:], sind.t.rearrange("(t p) f -> p t f", p=128), reads=[sind], writes=[sn])
    SC = 128.0 ** -0.5
    kb.op("dve", lambda e: e.tensor_scalar(out=csq[:], in0=cs[:], scalar1=SC, scalar2=0.0, op0=ALU.mult, op1=ALU.add), reads=[cs], writes=[csq])
    kb.op("dve", lambda e: e.tensor_scalar(out=snq[:], in0=sn[:], scalar1=SC, scalar2=0.0, op0=ALU.mult, op1=ALU.add), reads=[sn], writes=[snq])
    epsb = kb.sb("epsb", [128, 1])
    kb.op("dve", lambda e: e.memset(epsb[:], 1e-6), writes=[epsb])

    xts = Rot([kb.sb("xt%d" % i, [128, D]) for i in range(2)])
    ss = kb.sb("ss", [128, 1])
    rstd = kb.sb("rstd", [128, 1])
    hs = Rot([kb.sb("h%d" % i, [128, D]) for i in range(1)])
    tps = Rot([kb.ps("tp%d" % i, [128, 512]) for i in range(2)])
    ev = Eng2(["act", "dve"])
    hT = [kb.sb("hT%d" % t, [128, 16, 128], F32R) for t in range(NT)]
    for tt in range(NT):
        xt = xts.next()
        h = hs.next()
        kb.dma("sp", xt[:], x[tt * 128:(tt + 1) * 128, :], reads=[x], writes=[xt])
        rms_rstd(kb, xt, 128, D, junk, ss, epsb, rstd)
        kb.op("dve", lambda e, xt=xt, h=h: e.scalar_tensor_tensor(out=h[:], in0=xt[:], scalar=rstd[:, 0:1], in1=gsc[:], op0=ALU.mult, op1=ALU.mult),
              reads=[xt, rstd, gsc], writes=[h])
        kb.op("pool", lambda e, h=h: e.tensor_tensor(out=h[:], in0=h[:], in1=shb[:], op=ALU.add), reads=[h, shb], writes=[h])
        transpose_into(kb, h, 128, 16, hT[tt], ident, tps, ev)

    wss = Rot([kb.sb("ws%d" % i, [128, 16, 512], F32R) for i in range(2)])
    pss = Rot([kb.ps("mm%d" % i, [128, 512]) for i in range(4)])
    zts = Rot([kb.sb("zt%d" % i, [128, 512]) for i in range(3)])
    lfs = Rot([kb.sb("lf%d" % i, [128, 512]) for i in range(2)])
    ta = kb.sb("ta", [128, 4, 64])
    tb = kb.sb("tb", [128, 4, 64])
    sg = kb.sb("sg", [128, 512])
    oq = Rot(["sp", "act"])

    def rope(p, zt, nh, c_t, s_t):
        pv = p[:, 0:nh * 128].rearrange("p (h t d) -> p h t d", h=nh, t=2)
        zv = zt[:, 0:nh * 128].rearrange("p (h t d) -> p h t d", h=nh, t=2)
        cb_ = c_t.unsqueeze(1).broadcast_to([128, nh, 64])
        sb_ = s_t.unsqueeze(1).broadcast_to([128, nh, 64])
        t1, t2 = pv[:, :, 0, :], pv[:, :, 1, :]
        kb.op("dve", lambda e: e.tensor_tensor(out=ta[:, 0:nh, :], in0=t1, in1=cb_, op=ALU.mult), reads=[p, cs, sn, csq, snq], writes=[ta])
        kb.op("dve", lambda e: e.tensor_tensor(out=tb[:, 0:nh, :], in0=t2, in1=sb_, op=ALU.mult), reads=[p, cs, sn, csq, snq], writes=[tb])
        kb.op("dve", lambda e: e.tensor_tensor(out=zv[:, :, 0, :], in0=ta[:, 0:nh, :], in1=tb[:, 0:nh, :], op=ALU.subtract), reads=[ta, tb], writes=[zt], partial=True)
        kb.op("dve", lambda e: e.tensor_tensor(out=ta[:, 0:nh, :], in0=t2, in1=cb_, op=ALU.mult), reads=[p, cs, sn, csq, snq], writes=[ta])
        kb.op("dve", lambda e: e.tensor_tensor(out=tb[:, 0:nh, :], in0=t1, in1=sb_, op=ALU.mult), reads=[p, cs, sn, csq, snq], writes=[tb])
        kb.op("dve", lambda e: e.tensor_tensor(out=zv[:, :, 1, :], in0=ta[:, 0:nh, :], in1=tb[:, 0:nh, :], op=ALU.add), reads=[ta, tb], writes=[zt], partial=True)

    for (c0, wd, kind) in L1_BLOCKS:
        ws_ = wss.next()
        kb.dma("pool", ws_[:, :, 0:wd], w.t[:, c0:c0 + wd].rearrange("(kc p) n -> p kc n", p=128), reads=[w], writes=[ws_])
        for tt in range(NT):
            p = pss.next()
            for kc in range(16):
                kb.op("pe", lambda e, kc=kc, p=p, tt=tt: e.matmul(p[:, 0:wd], hT[tt][:, kc, :], ws_[:, kc, 0:wd], start=(kc == 0), stop=(kc == 15)),
                      reads=[hT[tt], ws_], writes=[p], inc=(kc == 15))
            zt = zts.next()
            if kind == "q":
                rope(p, zt, 4, csq[:, tt, :], snq[:, tt, :])
            elif kind == "kv":
                rope(p, zt, 2, cs[:, tt, :], sn[:, tt, :])
                kb.op("act", lambda e, p=p, zt=zt: e.activation(out=zt[:, 256:512], in_=p[:, 256:512], func=AF.Copy), reads=[p], writes=[zt], partial=True)
            elif kind == "gate":
                kb.op("act", lambda e, p=p, zt=zt: e.activation(out=zt[:, 0:wd], in_=p[:, 0:wd], func=AF.Sigmoid), reads=[p], writes=[zt])
            elif kind == "id":
                kb.op("act", lambda e, p=p, zt=zt: e.activation(out=zt[:, 0:wd], in_=p[:, 0:wd], func=AF.Copy), reads=[p], writes=[zt])
            elif kind == "silu":
                kb.op("act", lambda e, p=p, zt=zt: e.activation(out=zt[:, 0:wd], in_=p[:, 0:wd], func=AF.Silu), reads=[p], writes=[zt])
            elif kind == "sig":
                kb.op("act", lambda e, p=p, zt=zt: e.activation(out=zt[:, 0:wd], in_=p[:, 0:wd], func=AF.Sigmoid), reads=[p], writes=[zt])
            else:
                fo = c0 - 3608
                lf = lfs.next()
                kb.op("act", lambda e, p=p: e.activation(out=sg[:], in_=p[:], func=AF.Sigmoid, scale=-1.0), reads=[p], writes=[sg])
                kb.op("dve", lambda e, zt=zt: e.tensor_tensor(out=zt[:], in0=sg[:], in1=oml[:, fo:fo + 512], op=ALU.mult), reads=[sg, oml], writes=[zt])
                kb.op("act", lambda e, p=p: e.activation(out=sg[:], in_=p[:], func=AF.Sigmoid), reads=[p], writes=[sg])
                kb.op("dve", lambda e: e.tensor_tensor(out=sg[:], in0=sg[:], in1=oml[:, fo:fo + 512], op=ALU.mult), reads=[sg, oml], writes=[sg])
                kb.op("dve", lambda e: e.tensor_tensor(out=sg[:], in0=sg[:], in1=lbb[:, fo:fo + 512], op=ALU.add), reads=[sg, lbb], writes=[sg])
                kb.op("act", lambda e, lf=lf: e.activation(out=lf[:], in_=sg[:], func=AF.Ln), reads=[sg], writes=[lf])
                kb.dma(oq.next(), logf[tt * 128:(tt + 1) * 128, fo:fo + 512], lf[:], reads=[lf], writes=[logf], partial=True, sbuf_side=lf)
            kb.dma(oq.next(), z[tt * 128:(tt + 1) * 128, c0:c0 + wd], zt[:, 0:wd], reads=[zt], writes=[z], partial=True, sbuf_side=zt)
    kb.finish()
    return kb


def run_L1(kb, xs, adav, w, cos, sin, lbv):
    ident = np.eye(128, dtype=np.float32)
    maps = []
    for j in range(8):
        sl = slice(j * TPC, (j + 1) * TPC)
        maps.append({"x": np.ascontiguousarray(xs[sl]), "adav": adav, "w": w, "cosd": np.ascontiguousarray(cos[sl]),
                     "sind": np.ascontiguousarray(sin[sl]), "lbv": lbv, "ident": ident})
    res = kb.run(maps).results
    z = np.concatenate([r["z"] for r in res], axis=0)
    logf = np.concatenate([r["logf"] for r in res], axis=0)
    return z, logf


NEG = -30000.0
S_ALL = 8192
NKT = S_ALL // 128


def slot_tile(s, j):
    return 8 * s + (j if s % 2 == 0 else 7 - j)


def build_L2(nslots=8, nkt_all=NKT, stop=99):
    kb = KB()
    qz = kb.din("qz", [1024, 1024])
    gz = kb.din("gz", [1024, 24])
    kva = kb.din("kva", [S_ALL, 1536])
    pekc = kb.din("pekc", [32, 128])
    wkc = kb.din("wkc", [4096, 128])
    pevc = kb.din("pevc", [32, 128])
    wvc = kb.din("wvc", [4096, 128])
    identd = kb.din("ident", [128, 128])
    kposd = kb.din("kposc", [128, 128])
    tposrd = kb.din("tposr", [1, 1024])
    tposcd = kb.din("tposc", [128, 8])
    cmpld = kb.din("cmpl", [1, 512])
    ovld = kb.din("ovl", [512, 128])
    addmd = kb.din("addm", [8, 128, 128])
    validd = kb.din("validm", [8, 128, 128])
    attn = kb.dout("attn", [1024, 1024])

    ident = kb.sb("ident_s", [128, 128])
    identr = kb.sb("identr", [128, 128], F32R)
    i4 = kb.sb("i4", [128, 4, 128], F32R)
    onesr = kb.sb("onesr", [1, 128], F32R)
    kposc = kb.sb("kposc_s", [128, 128])
    tposr = kb.sb("tposr_s", [128, 1024])
    tposc = kb.sb("tposc_s", [128, 8])
    cmpl = kb.sb("cmpl_s", [128, 512])
    gates = kb.sb("gates", [128, 8, 24])
    kb.dma("sp", ident[:], identd[:], reads=[identd], writes=[ident])
    kb.dma("sp", kposc[:], kposd[:], reads=[kposd], writes=[kposc])
    kb.dma("sp", tposc[:], tposcd[:], reads=[tposcd], writes=[tposc])
    bcast_load(kb, "sp", tposr, tposrd.t[0, :], tposrd)
    bcast_load(kb, "sp", cmpl, cmpld.t[0, :], cmpld)
    kb.dma("sp", gates[:], gz.t.rearrange("(s p) c -> p s c", p=128), reads=[gz], writes=[gates])
    kb.op("dve", lambda e: e.tensor_copy(out=identr[:], in_=ident[:]), reads=[ident], writes=[identr])
    for h in range(4):
        kb.op("dve", lambda e, h=h: e.tensor_copy(out=i4[:, h, :], in_=ident[:]), reads=[ident], writes=[i4], partial=True)
    onesf = kb.sb("onesf", [1, 128])
    kb.op("dve", lambda e: e.memset(onesf[:], 1.0), writes=[onesf])
    kb.op("dve", lambda e: e.tensor_copy(out=onesr[:], in_=onesf[:]), reads=[onesf], writes=[onesr])

    KA = kb.sb("KA", [128, 16 * 514], F32R)
    KBf = kb.sb("KBf", [128, 16 * 514], F32R)
    KA2 = KA[:].rearrange("p (r c) -> p r c", r=16)
    KB2 = KBf[:].rearrange("p (r c) -> p r c", r=16)
    Vs = kb.sb("Vs", [128, NKT, 130], BF16)
    Vw = kb.sb("Vw", [128, NKT, 130], BF16)
    WX = kb.sb("WX", [128, 2, 32, 128], F32R)
    Wk = Wv = WX
    WkA = WX[:, 0]
    WvA = WX[:, 1]
    kcmpT = kb.sb("kcmpT", [128, 512], F32R)
    vco = kb.sb("vco", [128, 4, 256], F32R)
    kb.op("pool", lambda e: e.memset(Vs[:, :, 128:130], 1.0), writes=[Vs])
    kb.op("pool", lambda e: e.memset(Vw[:, :, 128:130], 1.0), writes=[Vw])
    ovs = kb.sb("ovs", [128, 4, 128])
    kb.dma("sp", ovs[:], ovld.t.rearrange("(nt p) m -> p nt m", p=128), reads=[ovld], writes=[ovs])

    tps = Rot([kb.ps("tp%d" % i, [128, 512]) for i in range(2)])
    ev = Eng2(["act", "dve"])
    kvl = Rot([kb.sb("kvl%d" % i, [128, 4, 128]) for i in range(3)])
    lq = Rot(["sp"])

    def load_cw():
        kb.dma("pool", WkA, wkc.t.rearrange("(l d) o -> d l o", d=128), reads=[wkc], writes=[WX])
        kb.dma("pool", WvA, wvc.t.rearrange("(l d) o -> d l o", d=128), reads=[wvc], writes=[WX], partial=True)
    load_cw()
    pes = kb.sb("pes", [32, 2, 128])
    kb.dma("sp", pes[:, 0, :], pekc[:], reads=[pekc], writes=[pes], partial=True)
    kb.dma("sp", pes[:, 1, :], pevc[:], reads=[pevc], writes=[pes], partial=True)
    peT = kb.sb("peT", [128, 2, 34], F32R)
    kb.op("dve", lambda e: e.memset(peT[:].bitcast(F32), 0.0), writes=[peT])
    tp = tps.next()
    for i in range(2):
        kb.op("pe", lambda e, i=i: e.transpose(tp[:, i * 32:(i + 1) * 32], pes[:, i, :], ident[:32, :32]), reads=[pes, ident], writes=[tp], inc=(i == 1), partial=(i > 0))
    copy_op(kb, "dve", peT[:, :, 0:32], tp[:, 0:64].rearrange("p (i l) -> p i l", i=2), reads=[tp], writes=[peT], partial=True)
    if stop == 1:
        kb.finish()
        return kb

    kbias = kb.sb("kbias", [128, 1])
    vbrow = kb.sb("vbrow", [1, 128], F32R)
    tp = tps.next()
    for l in range(32):
        kb.op("pe", lambda e, l=l: e.matmul(tp[:, 0:2], WkA[:, l, :], peT[:, 0, l:l + 2], start=(l == 0), stop=(l == 31)), reads=[Wk, peT], writes=[tp], inc=(l == 31))
    copy_op(kb, "dve", kbias[:], tp[:, 0:1], reads=[tp], writes=[kbias])
    if stop == 2:
        kb.finish()
        return kb

    tp = tps.next()
    for l in range(32):
        kb.op("pe", lambda e, l=l: e.matmul(tp[0:2, 0:128], peT[:, 1, l:l + 2], WvA[:, l, :], start=(l == 0), stop=(l == 31)), reads=[Wv, peT], writes=[tp], inc=(l == 31))
    copy_op(kb, "dve", vbrow[:], tp[0:1, 0:128], reads=[tp], writes=[vbrow])
    if stop == 3:
        kb.finish()
        return kb


    osb = [kb.ps("os%d" % i, [128, 512]) for i in range(2)]
    owb = [kb.ps("ow%d" % i, [128, 512]) for i in range(2)]
    for b_ in osb + owb:
        b_.t = b_.t[:, 0:260].rearrange("p (a b) -> p a b", a=2)
    pcp = Rot([kb.ps("pc%d" % i, [128, 512]) for i in range(2)])
    qls = Rot([kb.sb("ql%d" % i, [128, 512]) for i in range(2)])
    QTs = Rot([kb.sb("QT%d" % i, [128, 4, 128], F32R) for i in range(2)])
    pcs = Rot([kb.sb("pcs%d" % i, [128, 512]) for i in range(2)])
    pTs = Rot([kb.sb("pT%d" % i, [128, 4, 128], F32R) for i in range(2)])
    PTs = Rot([kb.sb("PT%d" % i, [128, 512], BF16) for i in range(3)])
    mb4 = Rot([kb.sb("mb4_%d" % i, [128, 4, 128], F32R) for i in range(3)])
    mtmp = Rot([kb.sb("mtmp%d" % i, [128, 128]) for i in range(2)])
    mtmp2 = Rot([kb.sb("mtmq%d" % i, [128, 128]) for i in range(2)])
    accs = Rot([kb.sb("acc%d" % i, [128, 512]) for i in range(2)])
    cmask = kb.sb("cmask", [128, 512])
    addm = Rot([kb.sb("addm%d" % i, [128, 128]) for i in range(2)])
    validm = Rot([kb.sb("valm%d" % i, [128, 128]) for i in range(2)])
    impa = kb.sb("impa", [128, 128])
    impw = kb.sb("impw", [128, 128])
    impw2 = kb.sb("impw2", [128, 128])
    mx8 = kb.sb("mx8", [128, 8])
    negm = kb.sb("negm", [128, 128])
    negmx = WX
    negmxA = WX[:].rearrange("p a l o -> p (a l o)").rearrange("p (m k) -> p m k", k=64)
    sm = kb.sb("sm", [128, 8])
    oq = Rot(["sp"])

    for g in range(2):
        if g > 0:
            load_cw()
        for kt in range(nkt_all):
            t_ = kvl.next()
            src = kva.t[kt * 128:(kt + 1) * 128, 0:512].rearrange("p (i gg d) -> p i gg d", i=2, gg=2)[:, :, g, :]
            kb.dma(lq.next(), t_[:, 0:2, :], src, reads=[kva], writes=[t_])
            tp = tps.next()
            for i in range(2):
                kb.op("pe", lambda e, i=i, t_=t_, tp=tp: e.transpose(tp[:, i * 128:(i + 1) * 128], t_[:, i, :], ident[:]), reads=[t_, ident], writes=[tp], inc=(i == 1), partial=(i > 0))
            import os
            E = os.environ.get("EXP", "")
            if E != "noact":
                copy_op(kb, "dve" if E == "alldve" else "act", KA2[:, :, 8 * kt:8 * kt + 8], tp[:, 0:128].rearrange("p (cc r) -> p r cc", r=16), reads=[tp], writes=[KA], partial=True)
            if E != "nodve":
                copy_op(kb, "dve", KB2[:, :, 8 * kt:8 * kt + 8], tp[:, 128:256].rearrange("p (cc r) -> p r cc", r=16), reads=[tp], writes=[KBf], partial=True)
        import os
        if os.environ.get("DBG") != "1":
            kb.op("dve", lambda e: e.memset(KA2[:, :, 512:514].bitcast(F32), 0.0), writes=[KA], partial=True)
            kb.op("dve", lambda e: e.memset(KB2[:, :, 512:514].bitcast(F32), 0.0), writes=[KBf], partial=True)
        if stop == 4:
            kb.finish()
            return kb

        tp = tps.next()
        nb = (nkt_all * 128 - 32) // 16 + 1
        for l in range(32):
            rhs = KA2[:, l % 16, l // 16:l // 16 + 512]
            kb.op("pe", lambda e, l=l, rhs=rhs: e.matmul(tp[:, 0:512], WkA[:, l, :], rhs, start=(l == 0), stop=(l == 31)), reads=[Wk, KA], writes=[tp], inc=(l == 31))
        kb.op("act", lambda e: e.activation(out=kcmpT[:, 0:512], in_=tp[:, 0:512], func=AF.Identity, bias=kbias[:]), reads=[tp, kbias], writes=[kcmpT])
        if stop == 5:
            kb.finish()
            return kb

        for nt in range(4):
            n0 = nt * 128
            nn = 128
            tp = tps.next()
            for l in range(32):
                lhsT = KB2[:, l % 16, n0 + l // 16:n0 + l // 16 + 128]
                kb.op("pe", lambda e, l=l, lhsT=lhsT: e.matmul(tp[0:nn, 0:128], lhsT, WvA[:, l, :], start=(l == 0), stop=False), reads=[Wv, KBf], writes=[tp], inc=False)
            kb.op("pe", lambda e: e.matmul(tp[0:nn, 0:128], onesr[:, 0:nn], vbrow[:], start=False, stop=True), reads=[onesr, vbrow], writes=[tp])
            copy_op(kb, "act", vco[0:nn, nt, 0:128], tp[0:nn, 0:128], reads=[tp], writes=[vco], partial=(nt > 0))
        kb.op("dve", lambda e: e.tensor_copy(out=vco[:, :, 128:256], in_=ovs[:]), reads=[ovs], writes=[vco], partial=True)
        if stop == 6:
            kb.finish()
            return kb


        for kt in range(nkt_all):
            t_ = kvl.next()
            src = kva.t[kt * 128:(kt + 1) * 128, 512:1536].rearrange("p (i gg d) -> p i gg d", i=4, gg=2)[:, :, g, :]
            kb.dma(lq.next(), t_[:], src, reads=[kva], writes=[t_])
            tp = tps.next()
            for i in range(2):
                kb.op("pe", lambda e, i=i, t_=t_, tp=tp: e.transpose(tp[:, i * 128:(i + 1) * 128], t_[:, 2 * i, :], ident[:]), reads=[t_, ident], writes=[tp], inc=(i == 1), partial=(i > 0))
            copy_op(kb, "act", KA[:, kt * 128:(kt + 1) * 128], tp[:, 0:128], reads=[tp], writes=[KA], partial=True)
            copy_op(kb, "dve", KBf[:, kt * 128:(kt + 1) * 128], tp[:, 128:256], reads=[tp], writes=[KBf], partial=True)
            kb.op("pool", lambda e, t_=t_, kt=kt: e.tensor_copy(out=Vs[:, kt, 0:128], in_=t_[:, 1, :]), reads=[t_], writes=[Vs], partial=True)
            kb.op("pool", lambda e, t_=t_, kt=kt: e.tensor_copy(out=Vw[:, kt, 0:128], in_=t_[:, 3, :]), reads=[t_], writes=[Vw], partial=True)

        for s in range(nslots):
            nk_sel = min(8 * s + 8, nkt_all)
            k_lo = max(0, 8 * s - 4)
            ql = qls.next()
            QT = QTs.next()
            acc = accs.next()
            kb.dma("sp", ql[:], qz[s * 128:(s + 1) * 128, g * 512:(g + 1) * 512], reads=[qz], writes=[ql])
            tp = tps.next()
            for h in range(4):
                kb.op("pe", lambda e, h=h, tp=tp, ql=ql: e.transpose(tp[:, h * 128:(h + 1) * 128], ql[:, h * 128:(h + 1) * 128], ident[:]), reads=[ql, ident], writes=[tp], inc=(h == 3), partial=(h > 0))
            copy_op(kb, "act", QT[:].rearrange("p h q -> p (h q)"), tp[:], reads=[tp], writes=[QT])
            QTf = QT[:].rearrange("p h q -> p (h q)")
            if g == 0 or True:
                am = addm.next()
                vm = validm.next()
                kb.dma("sp", am[:], addmd[s], reads=[addmd], writes=[am])
                kb.dma("sp", vm[:], validd[s], reads=[validd], writes=[vm])
                kb.op("dve", lambda e: e.tensor_scalar(out=cmask[:], in0=cmpl[:], scalar1=tposc[:, s:s + 1], scalar2=0.0, op0=ALU.is_le, op1=ALU.add), reads=[cmpl, tposc], writes=[cmask])

            def cmp_scores(h):
                sp_ = tps.next()
                kb.op("pe", lambda e: e.matmul(sp_[:], QT[:, h, :], kcmpT[:], start=True, stop=True), reads=[QT, kcmpT], writes=[sp_])
                return sp_

            def cmp_rest(h, sp_):
                hh = g * 4 + h
                kb.op("dve", lambda e: e.reduce_max(out=sm[:, 0:1], in_=sp_[:], axis=AX.X), reads=[sp_], writes=[sm], partial=True)
                kb.op("dve", lambda e: e.tensor_scalar(out=sm[:, 1:2], in0=sm[:, 0:1], scalar1=-1.0, scalar2=0.0, op0=ALU.mult, op1=ALU.add), reads=[sm], writes=[sm], partial=True)
                pc = pcs.next()
                kb.op("act", lambda e: e.activation(out=pc[:], in_=sp_[:], func=AF.Exp, bias=sm[:, 1:2]), reads=[sp_, sm], writes=[pc])
                kb.op("dve", lambda e: e.tensor_tensor(out=pc[:], in0=pc[:], in1=cmask[:], op=ALU.mult), reads=[pc, cmask], writes=[pc])
                kb.op("dve", lambda e: e.reduce_sum(out=sm[:, 2:3], in_=pc[:], axis=AX.X), reads=[pc], writes=[sm], partial=True)
                kb.op("dve", lambda e: e.tensor_scalar(out=sm[:, 2:3], in0=sm[:, 2:3], scalar1=1e-30, scalar2=0.0, op0=ALU.max, op1=ALU.add), reads=[sm], writes=[sm], partial=True)
                kb.op("dve", lambda e: e.reciprocal(out=sm[:, 3:4], in_=sm[:, 2:3]), reads=[sm], writes=[sm], partial=True)
                kb.op("dve", lambda e: e.tensor_scalar(out=pc[:], in0=pc[:], scalar1=sm[:, 3:4], scalar2=0.0, op0=ALU.mult, op1=ALU.add), reads=[pc, sm], writes=[pc])
                tpp = pcp.next()
                for nt in range(4):
                    kb.op("pe", lambda e, nt=nt: e.transpose(tpp[:, nt * 128:(nt + 1) * 128], pc[:, nt * 128:(nt + 1) * 128], ident[:]), reads=[pc, ident], writes=[tpp], inc=(nt == 3), partial=(nt > 0))
                pT = pTs.next()
                copy_op(kb, "act", pT[:].rearrange("p a b -> p (a b)"), tpp[:], reads=[tpp], writes=[pT])
                pcb = pcp.next()
                for nt in range(4):
                    kb.op("pe", lambda e, nt=nt: e.matmul(pcb[:, 0:256], pT[:, nt, :], vco[:, nt, :], start=(nt == 0), stop=(nt == 3)), reads=[pT, vco], writes=[pcb], inc=(nt == 3))
                kb.op("dve", lambda e: e.tensor_scalar(out=acc[:, h * 128:(h + 1) * 128], in0=pcb[:, 0:128], scalar1=gates[:, s, hh * 3:hh * 3 + 1], scalar2=0.0, op0=ALU.mult, op1=ALU.add),
                      reads=[pcb, gates], writes=[acc], partial=(h > 0))
                if h == 0:
                    kb.op("dve", lambda e: e.tensor_tensor(out=impa[:], in0=pcb[:, 128:256], in1=am[:], op=ALU.add), reads=[pcb, am], writes=[impa])
                else:
                    kb.op("dve", lambda e: e.tensor_tensor(out=impa[:], in0=pcb[:, 128:256], in1=impa[:], op=ALU.add), reads=[pcb, impa], writes=[impa])

            cur_c = cmp_scores(0)
            for h in range(4):
                nxt_c = cmp_scores(h + 1) if h < 3 else None
                cmp_rest(h, cur_c)
                cur_c = nxt_c
            kb.op("dve", lambda e: e.max(out=mx8[:], in_=impa[:]), reads=[impa], writes=[mx8])
            kb.op("dve", lambda e: e.match_replace(out=impw[:], in_to_replace=mx8[:], in_values=impa[:], imm_value=-3.0e38), reads=[mx8, impa], writes=[impw])
            kb.op("dve", lambda e: e.max(out=mx8[:], in_=impw[:]), reads=[impw], writes=[mx8])
            kb.op("dve", lambda e: e.match_replace(out=impw2[:], in_to_replace=mx8[:], in_values=impw[:], imm_value=-3.0e38), reads=[mx8, impw], writes=[impw2])
            kb.op("dve", lambda e: e.tensor_tensor(out=impw[:], in0=impa[:], in1=impw2[:], op=ALU.not_equal), reads=[impa, impw2], writes=[impw])
            kb.op("dve", lambda e: e.tensor_tensor(out=impw[:], in0=impw[:], in1=vm[:], op=ALU.mult), reads=[impw, vm], writes=[impw])
            nm_ = 2 * min(8 * s + 8, nkt_all)
            kb.op("dve", lambda e: e.tensor_scalar(out=negmxA[:, 0:nm_, :], in0=impw[:, 0:nm_].unsqueeze(2).broadcast_to([128, nm_, 64]), scalar1=-1.0, scalar2=-NEG, op0=ALU.add, op1=ALU.mult),
                  reads=[impw], writes=[negmx])

            def kbranch(KT, V, ob, kts, sel):
                nkt_ = len(kts)

                def scores(ii):
                    kt = kts[ii]
                    sp_ = tps.next()
                    kb.op("pe", lambda e: e.matmul(sp_[:], KT[:, kt * 128:(kt + 1) * 128], QTf, start=True, stop=False), reads=[KT, QT], writes=[sp_], inc=False)
                    posmask = (not sel) or (kt >= 8 * s)
                    if sel:
                        lhsT = negmxA[:, 2 * kt:2 * kt + 2, :].rearrange("p a b -> p (a b)")
                        kb.op("pe", lambda e: e.matmul(sp_[:], lhsT, i4[:].rearrange("p h q -> p (h q)"), start=False, stop=(not posmask)),
                              reads=[negmx, i4], writes=[sp_], inc=(not posmask), partial=True)
                    if posmask:
                        m4 = mb4.next()
                        ma = mtmp.next()
                        kb.op("dve", lambda e: e.tensor_scalar(out=ma[:], in0=tposr[:, s * 128:(s + 1) * 128], scalar1=kposc[:, kt:kt + 1], scalar2=0.0, op0=ALU.is_ge, op1=ALU.add),
                              reads=[tposr, kposc], writes=[ma])
                        if not sel:
                            mb_ = mtmp2.next()
                            kb.op("dve", lambda e: e.tensor_scalar(out=mb_[:], in0=tposr[:, s * 128:(s + 1) * 128], scalar1=kposc[:, 64 + kt:64 + kt + 1], scalar2=0.0, op0=ALU.is_lt, op1=ALU.add),
                                  reads=[tposr, kposc], writes=[mb_])
                            kb.op("dve", lambda e: e.tensor_tensor(out=ma[:], in0=ma[:], in1=mb_[:], op=ALU.mult), reads=[ma, mb_], writes=[ma])
                        kb.op("dve", lambda e: e.tensor_scalar(out=m4[:], in0=ma[:].unsqueeze(1).broadcast_to([128, 4, 128]), scalar1=-1.0, scalar2=-NEG, op0=ALU.add, op1=ALU.mult),
                              reads=[ma], writes=[m4])
                        kb.op("pe", lambda e: e.matmul(sp_[:], identr[:], m4[:].rearrange("p h q -> p (h q)"), start=False, stop=True), reads=[identr, m4], writes=[sp_], partial=True)
                    return sp_

                def pv(ii, sp_):
                    kt = kts[ii]
                    PT = PTs.next()
                    kb.op("act", lambda e: e.activation(out=PT[:], in_=sp_[:], func=AF.Exp), reads=[sp_], writes=[PT])
                    for h in range(4):
                        kb.op("pe", lambda e, h=h: e.matmul(ob[h // 2][:, h % 2, :], PT[:, h * 128:(h + 1) * 128], V[:, kt, :], start=(ii == 0 and h % 2 == 0), stop=(ii == nkt_ - 1), skip_group_check=True),
                              reads=[PT, V], writes=[ob[h // 2]], inc=(h == 3), partial=(not (ii == 0 and h % 2 == 0)))

                cur_sp = scores(0)
                for ii in range(nkt_):
                    nxt_sp = scores(ii + 1) if ii + 1 < nkt_ else None
                    pv(ii, cur_sp)
                    cur_sp = nxt_sp
                for h in range(4):
                    hh = g * 4 + h
                    o_ = ob[h // 2]
                    gcol = hh * 3 + (1 if sel else 2)
                    kb.op("dve", lambda e, o_=o_, h=h: e.tensor_scalar(out=sm[:, 4:5], in0=o_[:, h % 2, 128:129], scalar1=1e-30, scalar2=0.0, op0=ALU.max, op1=ALU.add), reads=[o_], writes=[sm], partial=True)
                    kb.op("dve", lambda e: e.reciprocal(out=sm[:, 5:6], in_=sm[:, 4:5]), reads=[sm], writes=[sm], partial=True)
                    kb.op("dve", lambda e, gcol=gcol: e.tensor_tensor(out=sm[:, 6:7], in0=sm[:, 5:6], in1=gates[:, s, gcol:gcol + 1], op=ALU.mult), reads=[sm, gates], writes=[sm], partial=True)
                    kb.op("dve", lambda e, o_=o_, h=h: e.scalar_tensor_tensor(out=acc[:, h * 128:(h + 1) * 128], in0=o_[:, h % 2, 0:128], scalar=sm[:, 6:7], in1=acc[:, h * 128:(h + 1) * 128], op0=ALU.mult, op1=ALU.add),
                          reads=[o_, sm, acc], writes=[acc], partial=True)

            kbranch(KBf, Vw, owb, list(range(k_lo, nk_sel)), sel=False)
            kbranch(KA, Vs, osb, list(range(0, nk_sel)), sel=True)
            kb.dma(oq.next(), attn[s * 128:(s + 1) * 128, g * 512:(g + 1) * 512], acc[:], reads=[acc], writes=[attn], partial=True, sbuf_side=acc)
    kb.finish()
    return kb


def l2_tables(j):
    tiles = [slot_tile(s, j) for s in range(8)]
    tpos = np.concatenate([np.arange(t * 128, (t + 1) * 128) for t in tiles]).astype(np.float32)
    tposr = tpos.reshape(1, 1024)
    tposc = np.ascontiguousarray(tpos.reshape(8, 128).T)
    addm = np.zeros((8, 128, 128), np.float32)
    validm = np.zeros((8, 128, 128), np.float32)
    m = np.arange(128)[None, :]
    for s, t in enumerate(tiles):
        tp = (t * 128 + np.arange(128))[:, None]
        cur = tp // 64
        a = np.zeros((128, 128), np.float32)
        a[np.broadcast_to(m > cur, (128, 128))] = -1e30
        a[np.broadcast_to(m == cur - 1, (128, 128))] = 1e30
        a[np.broadcast_to(m == cur, (128, 128))] = 2e30
        a[:, 0] = 3e30
        addm[s] = a
        validm[s] = (m <= cur).astype(np.float32)
    return tposr, tposc, addm, validm


def l2_consts():
    kpos = (np.arange(64)[None, :] * 128 + np.arange(128)[:, None]).astype(np.float32)
    kposc = np.concatenate([kpos, kpos + 512], axis=1)
    n = np.arange(512)
    cmpl = (16 * n + 31).astype(np.float32).reshape(1, 512)
    cmpl[0, 511] = 1e9
    cs = n[:, None] * 16
    ss = np.arange(128)[None, :] * 64
    ov = np.maximum(np.minimum(cs + 32, ss + 64) - np.maximum(cs, ss), 0).astype(np.float32) / 32
    ov[511] = 0
    return np.ascontiguousarray(kposc), cmpl, np.ascontiguousarray(ov)


def run_L2(kb, z, wl):
    ident = np.eye(128, dtype=np.float32)
    kposc, cmpl, ov = l2_consts()
    kva = np.ascontiguousarray(z[:, 1024:2560])
    maps = []
    rows_all = []
    for j in range(8):
        tiles = [slot_tile(s, j) for s in range(8)]
        rows = np.concatenate([np.arange(t * 128, (t + 1) * 128) for t in tiles])
        rows_all.append(rows)
        tposr, tposc, addm, validm = l2_tables(j)
        maps.append({"qz": np.ascontiguousarray(z[rows, 0:1024]), "gz": np.ascontiguousarray(z[rows, 2560:2584]), "kva": kva,
                     "pekc": wl["pe_kc"], "wkc": wl["w_kc"], "pevc": wl["pe_vc"], "wvc": wl["w_vc"], "ident": ident,
                     "kposc": kposc, "tposr": tposr, "tposc": tposc, "cmpl": cmpl, "ovl": ov, "addm": addm, "validm": validm})
    res = kb.run(maps).results
    attn = np.zeros((S_ALL, 1024), np.float32)
    for j in range(8):
        attn[rows_all[j]] = res[j]["attn"]
    return attn


def l3_consts():
    t = np.arange(128)
    ch = t // 32
    mid = ch * 32 + 15
    same = ch[:, None] == ch[None, :]
    L = ((t[:, None] <= t[None, :]) & same).astype(np.float32)
    M = ((t[:, None] <= mid[None, :]) & same).astype(np.float32)
    Lm = L - M
    Mid = np.zeros((128, 4), np.float32)
    for c in range(4):
        Mid[:, c] = ((ch == c) & (t <= c * 32 + 15)).astype(np.float32)
    lmx = np.concatenate([Lm, Mid], axis=1)
    rowmask = (ch[:, None] == np.arange(4)[None, :]).astype(np.float32)
    return np.ascontiguousarray(lmx), np.ascontiguousarray(L), np.ascontiguousarray(rowmask)


def build_L3(ntiles=NKT):
    kb = KB()
    qkgv = kb.din("qkgv", [S_ALL, 4, 128])
    lmxd = kb.din("lmx", [128, 132])
    maskd = kb.din("mask32", [128, 128])
    rowmd = kb.din("rowmask", [128, 4])
    identd = kb.din("ident", [128, 128])
    od = kb.dout("o", [S_ALL, 128])

    ident = kb.sb("ident_s", [128, 128])
    lmx = kb.sb("lmx_s", [128, 132])
    mask32 = kb.sb("mask_s", [128, 128])
    rowm = kb.sb("rowm_s", [128, 4])
    kb.dma("sp", ident[:], identd[:], reads=[identd], writes=[ident])
    kb.dma("sp", lmx[:], lmxd[:], reads=[lmxd], writes=[lmx])
    kb.dma("sp", mask32[:], maskd[:], reads=[maskd], writes=[mask32])
    kb.dma("sp", rowm[:], rowmd[:], reads=[rowmd], writes=[rowm])

    ins_ = Rot([kb.sb("in%d" % i, [128, 4, 128]) for i in range(3)])
    pA = Rot([kb.ps("pA%d" % i, [128, 512]) for i in range(2)])
    pB = Rot([kb.ps("pB%d" % i, [128, 512]) for i in range(2)])
    pC = kb.ps("pC", [128, 512])
    pDs = Rot([kb.ps("pD%d" % i, [128, 512]) for i in range(2)])
    pE = kb.ps("pE", [128, 512])
    ebs = Rot([kb.sb("eb%d" % i, [128, 3, 128]) for i in range(2)])
    ems = Rot([kb.sb("em%d" % i, [128, 3, 4]) for i in range(2)])
    qes = Rot([kb.sb("qe%d" % i, [128, 128], F32R) for i in range(2)])
    kes = Rot([kb.sb("ke%d" % i, [128, 128], F32R) for i in range(2)])
    qss = [Rot([kb.sb("qs%d_%d" % (c, i), [128, 128], F32R) for i in range(2)]) for c in range(4)]
    kms = [Rot([kb.sb("km%d_%d" % (c, i), [128, 128], F32R) for i in range(2)]) for c in range(4)]
    vrs = Rot([kb.sb("vr%d" % i, [128, 128], F32R) for i in range(2)])
    scs = Rot([kb.sb("sc%d" % i, [128, 128], F32R) for i in range(2)])
    oss = Rot([kb.sb("os%d" % i, [128, 128]) for i in range(2)])
    kets = Rot([kb.sb("ket%d" % i, [128, 128]) for i in range(2)])
    for c in range(4):
        for b_ in qss[c].items:
            kb.op("pool", lambda e, b_=b_: e.memset(b_[:].bitcast(F32), 0.0), writes=[b_])
    Sf = Rot([kb.sb("Sf%d" % i, [128, 128]) for i in range(3)])
    Sr = Rot([kb.sb("Sr%d" % i, [128, 128], F32R) for i in range(3)])
    tmp = kb.sb("stmp", [128, 128])
    s_prev_f = Sf.next()
    s_prev_r = Sr.next()
    kb.op("dve", lambda e: e.memset(s_prev_f[:], 0.0), writes=[s_prev_f])
    kb.op("dve", lambda e: e.memset(s_prev_r[:].bitcast(F32), 0.0), writes=[s_prev_r])

    def front(ti):
        it = ins_.next()
        kb.dma("sp", it[:], qkgv[ti * 128:(ti + 1) * 128], reads=[qkgv], writes=[it])
        a = pA.next()
        pD_ = pDs.next()
        b = pB.next()
        kb.op("pe", lambda e: e.matmul(a[:, 0:132], it[:, 2, :], lmx[:], start=True, stop=True), reads=[it, lmx], writes=[a], inc=False)
        kb.op("pe", lambda e: e.matmul(a[:, 256:384], lmx[:, 0:128], it[:, 2, :], start=True, stop=True), reads=[it, lmx], writes=[a], partial=True)
        kb.op("pe", lambda e: e.transpose(b[:, 0:128], it[:, 0, :], ident[:]), reads=[it, ident], writes=[b], inc=False)
        kb.op("pe", lambda e: e.transpose(b[:, 128:256], it[:, 1, :], ident[:]), reads=[it, ident], writes=[b], partial=True)
        eb = ebs.next()
        em = ems.next()
        kb.op("act", lambda e: e.activation(out=eb[:, 0, :], in_=a[:, 0:128], func=AF.Exp), reads=[a], writes=[eb])
        kb.op("act", lambda e: e.activation(out=em[:, 0, :], in_=a[:, 128:132], func=AF.Exp), reads=[a], writes=[em])
        kb.op("act", lambda e: e.activation(out=eb[:, 1, :], in_=a[:, 0:128], func=AF.Exp, scale=-1.0), reads=[a], writes=[eb], partial=True)
        kb.op("act", lambda e: e.activation(out=eb[:, 2, :], in_=a[:, 256:384], func=AF.Exp, scale=-1.0), reads=[a], writes=[eb], partial=True)

        kb.op("dve", lambda e: e.tensor_copy(out=em[:, 1, :], in_=eb[:, 0, 31:128:32]), reads=[eb], writes=[em], partial=True)
        kb.op("dve", lambda e: e.tensor_tensor(out=em[:, 2, :], in0=em[:, 0, :], in1=em[:, 1, :], op=ALU.mult), reads=[em], writes=[em], partial=True)

        qe = qes.next(); ke = kes.next(); vr = vrs.next(); sc = scs.next()
        kb.op("dve", lambda e: e.tensor_tensor(out=qe[:], in0=b[:, 0:128], in1=eb[:, 0, :], op=ALU.mult), reads=[b, eb], writes=[qe])
        kb.op("dve", lambda e: e.tensor_tensor(out=ke[:], in0=b[:, 128:256], in1=eb[:, 1, :], op=ALU.mult), reads=[b, eb], writes=[ke])
        kb.op("pool", lambda e: e.tensor_copy(out=vr[:], in_=it[:, 3, :]), reads=[it], writes=[vr])

        qs = [qss[c].next() for c in range(4)]
        km = [kms[c].next() for c in range(4)]
        ket = kets.next()
        kb.op("pool", lambda e: e.tensor_tensor(out=ket[:], in0=it[:, 1, :], in1=eb[:, 2, :], op=ALU.mult), reads=[it, eb], writes=[ket])
        for c in range(4):
            kb.op("act", lambda e, c=c: e.activation(out=km[c][:], in_=ket[:], func=AF.Copy, scale=rowm[:, c:c + 1]), reads=[ket, rowm], writes=[km[c]])
            kb.op("act", lambda e, c=c: e.activation(out=qs[c][:, 32 * c:32 * c + 32], in_=qe[:, 32 * c:32 * c + 32], func=AF.Copy, scale=em[:, 0, c:c + 1]),
                  reads=[qe, em], writes=[qs[c]], partial=True)
        kb.op("pe", lambda e: e.matmul(pC[:, 0:128], ke[:], qe[:], start=True, stop=True), reads=[ke, qe], writes=[pC])
        kb.op("dve", lambda e: e.tensor_tensor(out=sc[:], in0=pC[:, 0:128], in1=mask32[:], op=ALU.mult), reads=[pC, mask32], writes=[sc])

        for c in range(4):
            kb.op("pe", lambda e, c=c: e.matmul(pD_[:, 128 * c:128 * c + 128], km[c][:], vr[:], start=True, stop=True), reads=[km[c], vr], writes=[pD_], inc=(c == 3), partial=(c > 0))
        return dict(qs=qs, vr=vr, sc=sc, em=em, pD=pD_)

    def back(ti, h_):
        nonlocal s_prev_f, s_prev_r
        qs = h_['qs']; vr = h_['vr']; sc = h_['sc']; em = h_['em']; pD_ = h_['pD']
        kb.op("pe", lambda e: e.matmul(pE[:, 0:128], sc[:], vr[:], start=True, stop=False), reads=[sc, vr], writes=[pE], inc=False)
        for c in range(4):
            kb.op("pe", lambda e, c=c: e.matmul(pE[:, 0:128], qs[c][:], s_prev_r[:], start=False, stop=(c == 3)), reads=[qs[c], s_prev_r], writes=[pE], partial=True)
            sf = Sf.next(); sr = Sr.next()
            kb.op("dve", lambda e, c=c: e.tensor_scalar(out=tmp[:], in0=s_prev_f[:], scalar1=em[:, 2, c:c + 1], scalar2=0.0, op0=ALU.mult, op1=ALU.add), reads=[s_prev_f, em], writes=[tmp])
            kb.op("dve", lambda e, c=c, sf=sf: e.scalar_tensor_tensor(out=sf[:], in0=pD_[:, 128 * c:128 * c + 128], scalar=em[:, 1, c:c + 1], in1=tmp[:], op0=ALU.mult, op1=ALU.add), reads=[pD_, em, tmp], writes=[sf])
            kb.op("act", lambda e, sf=sf, sr=sr: e.activation(out=sr[:], in_=sf[:], func=AF.Copy), reads=[sf], writes=[sr])
            s_prev_f, s_prev_r = sf, sr
        os_ = oss.next()
        kb.op("act", lambda e: e.activation(out=os_[:], in_=pE[:, 0:128], func=AF.Copy), reads=[pE], writes=[os_])
        kb.dma("sp", od[ti * 128:(ti + 1) * 128, :], os_[:], reads=[os_], writes=[od], partial=True, sbuf_side=os_)

    cur_h = front(0)
    for ti in range(ntiles):
        nxt_h = front(ti + 1) if ti + 1 < ntiles else None
        back(ti, cur_h)
        cur_h = nxt_h
    kb.finish()
    return kb


def run_L3(kb, z, logf):
    ident = np.eye(128, dtype=np.float32)
    lmx, L, rowmask = l3_consts()
    maps = []
    for j in range(8):
        c = slice(j * 128, (j + 1) * 128)
        qkgv = np.stack([z[:, 2584:3608][:, c], z[:, 3608:4632][:, c], logf[:, c], z[:, 4632:5656][:, c]], axis=1)
        maps.append({"qkgv": np.ascontiguousarray(qkgv), "lmx": lmx, "mask32": L, "rowmask": rowmask, "ident": ident})
    res = kb.run(maps).results
    return np.concatenate([r["o"] for r in res], axis=1)


def build_L4a():
    kb = KB()
    TH = 512
    x = kb.din("x", [TPC, D])
    attnT = kb.din("attnT", [1024, TPC])
    hg0T = kb.din("hg0T", [1024, TPC])
    ogT = kb.din("ogT", [1024, TPC])
    mgT = kb.din("mgT", [4096, TPC])
    ghg = kb.din("ghg", [128, 8])
    vec = kb.din("vec", [2, D])
    wba = kb.din("wba", [1024, D])
    wbh = kb.din("wbh", [1024, D])
    wo = kb.din("wo", [D, D])
    x1 = kb.dout("x1", [TPC, D])

    U0 = kb.sb("U0", [128, 4096], F32R)
    U1 = kb.sb("U1", [128, 4096], F32R)
    aT = U0[:].rearrange("p (k t) -> p k t", k=8)
    hT = U1[:].rearrange("p (k t) -> p k t", k=8)
    MT = kb.sb("MT", [128, 16, TH], F32R)
    ghs = kb.sb("ghs", [128, 8])
    gpg = kb.sb("gpg", [128, D])
    gtb = kb.sb("gtb", [128, D])
    epsb = kb.sb("epsb", [128, 1])
    onesf = kb.sb("onesf", [128, 128])
    onesr = kb.sb("onesr", [128, 128], F32R)
    kb.op("dve", lambda e: e.memset(epsb[:], 1e-6), writes=[epsb])
    kb.op("dve", lambda e: e.memset(onesf[:], 1.0), writes=[onesf])
    kb.op("dve", lambda e: e.tensor_copy(out=onesr[:], in_=onesf[:]), reads=[onesf], writes=[onesr])
    kb.dma("sp", ghs[:], ghg[:], reads=[ghg], writes=[ghs])
    bcast_load(kb, "sp", gtb, vec.t[0, :], vec)
    bcast_load(kb, "sp", gpg, vec.t[1, :], vec)
    kb.op("dve", lambda e: e.tensor_tensor(out=gpg[:], in0=gpg[:], in1=gtb[:], op=ALU.mult), reads=[gpg, gtb], writes=[gpg])
    pss = Rot([kb.ps("ps%d" % i, [128, 512]) for i in range(6)])
    h0s = Rot([kb.sb("h0_%d" % i, [128, TH]) for i in range(2)])
    ogs = Rot([kb.sb("og_%d" % i, [128, TH]) for i in range(2)])
    sqs = Rot([kb.sb("sq_%d" % i, [128, TH], F32R) for i in range(2)])
    rs = Rot([kb.sb("r_%d" % i, [128, TH]) for i in range(2)])
    was = Rot([kb.sb("wa_%d" % i, [128, 8, 128], F32R) for i in range(2)])
    wbs = Rot([kb.sb("wb_%d" % i, [128, 8, 128], F32R) for i in range(2)])
    gas = Rot([kb.sb("ga_%d" % i, [128, TH]) for i in range(2)])
    gbs = Rot([kb.sb("gb_%d" % i, [128, TH]) for i in range(2)])
    t1s = Rot([kb.sb("t1_%d" % i, [128, TH]) for i in range(2)])
    t2s = Rot([kb.sb("t2_%d" % i, [128, TH]) for i in range(2)])
    wos = Rot([kb.sb("wo_%d" % i, [128, 16, 256], F32R) for i in range(2)])
    xts = Rot([kb.sb("xt%d" % i, [128, D]) for i in range(2)])
    ots = Rot([kb.sb("ot%d" % i, [128, D]) for i in range(2)])
    ss = kb.sb("ss", [128, 1]); rstd = kb.sb("rstd", [128, 1])
    ev = Eng2(["act", "dve"])
    Y = [U0[:].rearrange("p (t c) -> p t c", t=2), U1[:].rearrange("p (t c) -> p t c", t=2)]
    YB = [U0, U1]

    for half in range(TPC // TH):
        tsl = slice(half * TH, (half + 1) * TH)
        kb.dma("pool", aT, attnT.t[:, tsl].rearrange("(k p) t -> p k t", p=128), reads=[attnT], writes=[U0])
        for kc in range(8):
            h0 = h0s.next(); og = ogs.next(); sq = sqs.next()
            kb.dma("sp", h0[:], hg0T[kc * 128:(kc + 1) * 128, tsl], reads=[hg0T], writes=[h0])
            kb.dma("sp", og[:], ogT[kc * 128:(kc + 1) * 128, tsl], reads=[ogT], writes=[og])
            kb.op("act", lambda e: e.activation(out=sq[:], in_=h0[:], func=AF.Square), reads=[h0], writes=[sq])
            p = pss.next()
            kb.op("pe", lambda e: e.matmul(p[:], onesr[:], sq[:], start=True, stop=True), reads=[onesr, sq], writes=[p])
            r = rs.next()
            kb.op("act", lambda e: e.activation(out=r[:], in_=p[:], func=AF.Sqrt, bias=epsb[:], scale=1.0 / 128), reads=[p, epsb], writes=[r])
            kb.op("dve", lambda e: e.reciprocal(out=r[:], in_=r[:]), reads=[r], writes=[r])
            kb.op("dve", lambda e: e.scalar_tensor_tensor(out=r[:], in0=h0[:], scalar=ghs[:, kc:kc + 1], in1=r[:], op0=ALU.mult, op1=ALU.mult), reads=[h0, ghs, r], writes=[r])
            kb.op("pool", lambda e: e.tensor_tensor(out=hT[:, kc, :], in0=r[:], in1=og[:], op=ALU.mult), reads=[r, og], writes=[U1], partial=(kc > 0))
        for cc in range(16):
            wa = was.next(); wb_ = wbs.next(); ga = gas.next(); gb_ = gbs.next()
            kb.dma("pool", wa[:], wba.t[:, cc * 128:(cc + 1) * 128].rearrange("(k p) n -> p k n", p=128), reads=[wba], writes=[wa])
            kb.dma("pool", wb_[:], wbh.t[:, cc * 128:(cc + 1) * 128].rearrange("(k p) n -> p k n", p=128), reads=[wbh], writes=[wb_])
            kb.dma("sp", ga[:], mgT[cc * 128:(cc + 1) * 128, tsl], reads=[mgT], writes=[ga])
            kb.dma("sp", gb_[:], mgT[2048 + cc * 128:2048 + (cc + 1) * 128, tsl], reads=[mgT], writes=[gb_])
            pa = pss.next()
            for kc in range(8):
                kb.op("pe", lambda e, kc=kc: e.matmul(pa[:], wa[:, kc, :], aT[:, kc, :], start=(kc == 0), stop=(kc == 7)), reads=[wa, U0], writes=[pa], inc=(kc == 7))
            pb = pss.next()
            for kc in range(8):
                kb.op("pe", lambda e, kc=kc: e.matmul(pb[:], wb_[:, kc, :], hT[:, kc, :], start=(kc == 0), stop=(kc == 7)), reads=[wb_, U1], writes=[pb], inc=(kc == 7))
            t1 = t1s.next(); t2 = t2s.next()
            kb.op("dve", lambda e: e.tensor_tensor(out=t1[:], in0=pa[:], in1=ga[:], op=ALU.mult), reads=[pa, ga], writes=[t1])
            kb.op("dve", lambda e: e.tensor_tensor(out=t2[:], in0=pb[:], in1=gb_[:], op=ALU.mult), reads=[pb, gb_], writes=[t2])
            kb.op("pool", lambda e: e.tensor_tensor(out=MT[:, cc, :], in0=t1[:], in1=t2[:], op=ALU.add), reads=[t1, t2], writes=[MT], partial=(cc > 0))
        for cb in range(8):
            w_ = wos.next()
            kb.dma("pool", w_[:], wo.t[:, cb * 256:(cb + 1) * 256].rearrange("(k p) n -> p k n", p=128), reads=[wo], writes=[w_])
            for tt in range(4):
                p = pss.next()
                for kc in range(16):
                    kb.op("pe", lambda e, kc=kc: e.matmul(p[:, 0:256], MT[:, kc, tt * 128:(tt + 1) * 128], w_[:, kc, :], start=(kc == 0), stop=(kc == 15)), reads=[MT, w_], writes=[p], inc=(kc == 15))
                copy_op(kb, ev.next(), Y[tt // 2][:, tt % 2, cb * 256:(cb + 1) * 256], p[:, 0:256], reads=[p], writes=[YB[tt // 2]], partial=not (cb == 0 and tt % 2 == 0))
        for tt in range(4):
            xt = xts.next(); ot = ots.next()
            yv = Y[tt // 2][:, tt % 2, :]
            r0 = half * TH + tt * 128
            kb.dma("sp", xt[:], x[r0:r0 + 128, :], reads=[x], writes=[xt])
            kb.op("act", lambda e: e.activation(out=ot[:], in_=yv, func=AF.Square, accum_out=ss[:]), reads=[YB[tt // 2]], writes=[ot, ss])
            kb.op("act", lambda e: e.activation(out=rstd[:], in_=ss[:], func=AF.Sqrt, bias=epsb[:], scale=1.0 / D), reads=[ss, epsb], writes=[rstd])
            kb.op("dve", lambda e: e.reciprocal(out=rstd[:], in_=rstd[:]), reads=[rstd], writes=[rstd])
            kb.op("dve", lambda e: e.scalar_tensor_tensor(out=ot[:], in0=yv, scalar=rstd[:, 0:1], in1=gpg[:], op0=ALU.mult, op1=ALU.mult), reads=[YB[tt // 2], rstd, gpg], writes=[ot])
            kb.op("pool", lambda e: e.tensor_tensor(out=ot[:], in0=ot[:], in1=xt[:], op=ALU.add), reads=[ot, xt], writes=[ot])
            kb.dma("sp", x1[r0:r0 + 128, :], ot[:], reads=[ot], writes=[x1], partial=True, sbuf_side=ot)
    kb.finish()
    return kb


def run_L4a(kb, xs, attn, hg0, z, ada_l, wl):
    maps = []
    ghg = np.ascontiguousarray(wl["g_hg_norm"].reshape(8, 128).T)
    vec = np.ascontiguousarray(np.stack([ada_l[4096:6144], wl["g_post_mix"]]))
    for j in range(8):
        sl = slice(j * TPC, (j + 1) * TPC)
        maps.append({"x": np.ascontiguousarray(xs[sl]), "attnT": np.ascontiguousarray(attn[sl].T), "hg0T": np.ascontiguousarray(hg0[sl].T),
                     "ogT": np.ascontiguousarray(z[sl, 5656:6680].T), "mgT": np.ascontiguousarray(z[sl, 6680:10776].T),
                     "ghg": ghg, "vec": vec, "wba": wl["w_br_attn"], "wbh": wl["w_br_hgrn"], "wo": wl["w_out"]})
    res = kb.run(maps).results
    return np.concatenate([r["x1"] for r in res], axis=0)


DFF = 5632


def build_L4b():
    kb = KB()
    TH = 512
    x1 = kb.din("x1", [TPC, D])
    xh = kb.din("xh", [2, D])
    hmaskd = kb.din("hmask", [128, 1])
    vec = kb.din("vec", [5, D])
    cwd = kb.din("cw", [128, 88, 4])
    wup = kb.din("wup", [D, 2 * DFF])
    wdn = kb.din("wdn", [DFF, D])
    identd = kb.din("ident", [128, 128])
    x2 = kb.dout("x2", [TPC, D])

    ident = kb.sb("ident_s", [128, 128])
    hmask = kb.sb("hmask_s", [128, 1])
    cw = kb.sb("cw_s", [128, 88, 4])
    epsb = kb.sb("epsb", [128, 1])
    shb = kb.sb("shb", [128, D])
    gsc = kb.sb("gsc", [128, D])
    gpg = shb
    kb.dma("sp", ident[:], identd[:], reads=[identd], writes=[ident])
    kb.dma("sp", hmask[:], hmaskd[:], reads=[hmaskd], writes=[hmask])
    kb.dma("sp", cw[:], cwd[:], reads=[cwd], writes=[cw])
    kb.op("dve", lambda e: e.memset(epsb[:], 1e-6), writes=[epsb])
    xtb = kb.sb("xt0", [128, D])
    hb = kb.sb("hb", [128, D])
    bcast_load(kb, "sp", shb, vec.t[0, :], vec)
    bcast_load(kb, "sp", gsc, vec.t[1, :], vec)
    bcast_load(kb, "sp", hb, vec.t[2, :], vec)
    kb.op("dve", lambda e: e.scalar_tensor_tensor(out=gsc[:], in0=gsc[:], scalar=1.0, in1=hb[:], op0=ALU.add, op1=ALU.mult), reads=[gsc, hb], writes=[gsc])

    H2T = kb.sb("H2T", [128, 16, TPC], BF16)
    Y = H2T[:].rearrange("p k t -> p (k t)").bitcast(F32).rearrange("p (t c) -> p t c", t=4)
    H2Th = kb.sb("H2Th", [128, 16, 2], BF16)
    aT = kb.sb("aT", [128, 44, TPC], BF16)
    wgs = Rot([kb.sb("wg%d" % i, [128, 16, 128], BF16) for i in range(2)])
    wvs = Rot([kb.sb("wv%d" % i, [128, 16, 128], BF16) for i in range(2)])
    wds = Rot([kb.sb("wd%d" % i, [128, 44, 128], BF16) for i in range(2)])
    ugs = Rot([kb.sb("ug%d" % i, [128, 2 + TPC]) for i in range(1)])
    uvs = Rot([kb.sb("uv%d" % i, [128, 2 + TPC]) for i in range(1)])
    tg = kb.sb("tg", [128, TPC]); sg = tg
    tv = ugs.items[0]
    ss = kb.sb("ss", [128, 1]); rstd = kb.sb("rstd", [128, 1])
    tps = Rot([kb.ps("tp%d" % i, [128, 512]) for i in range(2)])
    pss = Rot([kb.ps("mm%d" % i, [128, 512]) for i in range(4)])
    phs = Rot([kb.ps("ph%d" % i, [128, 512]) for i in range(2)])
    ev = Eng2(["act", "dve"])

    def norm_mod_T(xt, np_, dstT, col0):
        rms_rstd(kb, xt, np_, D, hb, ss, epsb, rstd)
        kb.op("dve", lambda e: e.scalar_tensor_tensor(out=hb[:np_, :], in0=xt[:np_, :], scalar=rstd[:np_, 0:1], in1=gsc[:np_, :], op0=ALU.mult, op1=ALU.mult),
              reads=[xt, rstd, gsc], writes=[hb])
        kb.op("pool", lambda e: e.tensor_tensor(out=hb[:np_, :], in0=hb[:np_, :], in1=shb[:np_, :], op=ALU.add), reads=[hb, shb], writes=[hb])
        transpose_into(kb, hb, np_, 16, dstT, ident, tps, ev, dst_col0=col0)

    xt = xtb
    kb.dma("sp", xt[0:2, :], xh[:], reads=[xh], writes=[xt])
    norm_mod_T(xt, 2, H2Th, 0)
    for tt in range(NT):
        kb.dma("sp", xt[:], x1[tt * 128:(tt + 1) * 128, :], reads=[x1], writes=[xt])
        norm_mod_T(xt, 128, H2T, tt * 128)
    bcast_load(kb, "sp", gpg, vec.t[3, :], vec)
    bcast_load(kb, "sp", gsc, vec.t[4, :], vec)
    kb.op("dve", lambda e: e.tensor_tensor(out=gpg[:], in0=gpg[:], in1=gsc[:], op=ALU.mult), reads=[gpg, gsc], writes=[gpg])
    for c in range(44):
        wg = wgs.next(); wv = wvs.next()
        kb.dma("pool", wg[:], wup.t[:, c * 128:(c + 1) * 128].rearrange("(k p) n -> p k n", p=128), reads=[wup], writes=[wg])
        kb.dma("pool", wv[:], wup.t[:, DFF + c * 128:DFF + (c + 1) * 128].rearrange("(k p) n -> p k n", p=128), reads=[wup], writes=[wv])
        outs = []
        for (w_, ubr, tbb, ch) in ((wg, ugs, tg, c), (wv, uvs, tv, 44 + c)):
            ub = ubr.next()
            tb = tbb[:, 0:TPC]
            ph = phs.next()
            for kc in range(16):
                kb.op("pe", lambda e, kc=kc: e.matmul(ph[:, 0:2], w_[:, kc, :], H2Th[:, kc, :], start=(kc == 0), stop=(kc == 15)), reads=[w_, H2Th], writes=[ph], inc=(kc == 15))
            kb.op("dve", lambda e: e.tensor_scalar(out=ub[:, 0:2], in0=ph[:, 0:2], scalar1=hmask[:, 0:1], scalar2=0.0, op0=ALU.mult, op1=ALU.add), reads=[ph, hmask], writes=[ub])
            for half in range(2):
                p = pss.next()
                for kc in range(16):
                    kb.op("pe", lambda e, kc=kc: e.matmul(p[:, 0:TH], w_[:, kc, :], H2T[:, kc, half * TH:(half + 1) * TH], start=(kc == 0), stop=(kc == 15)), reads=[w_, H2T], writes=[p], inc=(kc == 15))
                copy_op(kb, "act" if half == 0 else "dve", ub[:, 2 + half * TH:2 + (half + 1) * TH], p[:, 0:TH], reads=[p], writes=[ub], partial=True)
            kb.op("act", lambda e: e.activation(out=tb, in_=ub[:, 2:2 + TPC], func=AF.Identity, bias=cw[:, ch, 3:4], scale=cw[:, ch, 2:3]), reads=[ub, cw], writes=[tbb])
            kb.op("dve", lambda e: e.scalar_tensor_tensor(out=tb, in0=ub[:, 1:1 + TPC], scalar=cw[:, ch, 1:2], in1=tb, op0=ALU.mult, op1=ALU.add), reads=[ub, cw, tbb], writes=[tbb])
            kb.op("dve", lambda e: e.scalar_tensor_tensor(out=tb, in0=ub[:, 0:TPC], scalar=cw[:, ch, 0:1], in1=tb, op0=ALU.mult, op1=ALU.add), reads=[ub, cw, tbb], writes=[tbb])
        kb.op("act", lambda e: e.activation(out=sg[:], in_=tg[:], func=AF.Silu), reads=[tg], writes=[sg])
        kb.op("pool", lambda e: e.tensor_tensor(out=aT[:, c, :], in0=sg[:], in1=tv[:, 0:TPC], op=ALU.mult), reads=[sg, tv], writes=[aT], partial=(c > 0))
    for half in range(TPC // TH):
        T0 = half * TH
        for cb in range(16):
            wd = wds.next()
            kb.dma("pool", wd[:], wdn.t[:, cb * 128:(cb + 1) * 128].rearrange("(k p) n -> p k n", p=128), reads=[wdn], writes=[wd])
            for tt in range(4):
                p = pss.next()
                for kc in range(44):
                    kb.op("pe", lambda e, kc=kc: e.matmul(p[:, 0:128], aT[:, kc, T0 + tt * 128:T0 + (tt + 1) * 128], wd[:, kc, :], start=(kc == 0), stop=(kc == 43)), reads=[aT, wd], writes=[p], inc=(kc == 43))
                copy_op(kb, ev.next(), Y[:, tt, cb * 128:(cb + 1) * 128], p[:, 0:128], reads=[p], writes=[H2T], partial=not (cb == 0 and tt == 0))
        for tt in range(4):
            yv = Y[:, tt, :]
            r0 = T0 + tt * 128
            kb.dma("sp", xt[:], x1[r0:r0 + 128, :], reads=[x1], writes=[xt])
            kb.op("act", lambda e: e.activation(out=hb[:], in_=yv, func=AF.Square, accum_out=ss[:]), reads=[H2T], writes=[hb, ss])
            kb.op("act", lambda e: e.activation(out=rstd[:], in_=ss[:], func=AF.Sqrt, bias=epsb[:], scale=1.0 / D), reads=[ss, epsb], writes=[rstd])
            kb.op("dve", lambda e: e.reciprocal(out=rstd[:], in_=rstd[:]), reads=[rstd], writes=[rstd])
            kb.op("dve", lambda e: e.scalar_tensor_tensor(out=hb[:], in0=yv, scalar=rstd[:, 0:1], in1=gpg[:], op0=ALU.mult, op1=ALU.mult), reads=[H2T, rstd, gpg], writes=[hb])
            kb.op("pool", lambda e: e.tensor_tensor(out=hb[:], in0=hb[:], in1=xt[:], op=ALU.add), reads=[hb, xt], writes=[hb])
            kb.dma("sp", x2[r0:r0 + 128, :], hb[:], reads=[hb], writes=[x2], partial=True, sbuf_side=hb)
    kb.finish()
    return kb


def run_L4b(kb, x1, ada_l, wl):
    ident = np.eye(128, dtype=np.float32)
    vec = np.ascontiguousarray(np.stack([ada_l[6144:8192], ada_l[8192:10240], wl["g_pre_ffn"], ada_l[10240:12288], wl["g_post_ffn"]]))
    cwf = np.concatenate([wl["conv_w"], wl["conv_b"][None, :]], axis=0)
    cw = np.ascontiguousarray(cwf.reshape(4, 88, 128).transpose(2, 1, 0))
    maps = []
    for j in range(8):
        sl = slice(j * TPC, (j + 1) * TPC)
        xh = np.ascontiguousarray(x1[j * TPC - 2:j * TPC]) if j > 0 else np.zeros((2, D), np.float32)
        hm = np.full((128, 1), 1.0 if j > 0 else 0.0, np.float32)
        maps.append({"x1": np.ascontiguousarray(x1[sl]), "xh": xh, "hmask": hm, "vec": vec, "cw": cw,
                     "wup": wl["w_up"], "wdn": wl["w_down"], "ident": ident})
    res = kb.run(maps).results
    return np.concatenate([r["x2"] for r in res], axis=0)


_PROGS = {}


def _prog(name, fn):
    if name not in _PROGS:
        _PROGS[name] = fn()
    return _PROGS[name]


def kernel(x, c, positions, w_ada, b_ada, g_pre_mix, w_in, pe_kc, w_kc, pe_vc, w_vc, lb_logits, g_hg_norm,
           w_br_attn, w_br_hgrn, w_out, g_post_mix, g_pre_ffn, w_up, conv_w, conv_b, w_down, g_post_ffn):
    f = lambda a: np.ascontiguousarray(np.asarray(a, dtype=np.float32))
    inp = {"c": f(c), "w_ada": f(w_ada), "b_ada": f(b_ada), "lb_logits": f(lb_logits), "positions": np.asarray(positions)}
    ada, lb, cos, sin = run_L0(inp)
    xs = f(x)[0]
    for l in range(4):
        adav = np.ascontiguousarray(np.stack([ada[l, 0:2048], ada[l, 2048:4096], f(g_pre_mix[l])]))
        z, logf = run_L1(_prog("L1", build_L1), xs, adav, f(w_in[l]), cos, sin, np.ascontiguousarray(lb[l:l + 1]))
        wl = {"pe_kc": f(pe_kc[l]), "w_kc": f(w_kc[l]), "pe_vc": f(pe_vc[l]), "w_vc": f(w_vc[l])}
        attn = run_L2(_prog("L2", build_L2), z, wl)
        hg0 = run_L3(_prog("L3", build_L3), z, logf)
        wl = {"g_hg_norm": f(g_hg_norm[l]), "g_post_mix": f(g_post_mix[l]), "w_br_attn": f(w_br_attn[l]),
              "w_br_hgrn": f(w_br_hgrn[l]), "w_out": f(w_out[l])}
        x1 = run_L4a(_prog("L4a", build_L4a), xs, attn, hg0, z, ada[l], wl)
        del z, logf, attn, hg0
        wl = {"g_pre_ffn": f(g_pre_ffn[l]), "g_post_ffn": f(g_post_ffn[l]), "conv_w": f(conv_w[l]), "conv_b": f(conv_b[l]),
              "w_up": f(w_up[l]), "w_down": f(w_down[l])}
        xs = run_L4b(_prog("L4b", build_L4b), x1, ada[l], wl)
    return xs.reshape(1, S_ALL, D).astype(np.float32)
```
